# Optimizing a Trainium2 kernel written in Bass

```python
import jax, jax.numpy as jnp
from jax import lax
import numpy as np

D_MODEL = 2048
BATCH = 2
SEQ = 16384
DEPTH = 1

HEAD_DIM = 128
A_Q_HEADS = 8
A_KV_HEADS = 2
B_HEADS = 8
DIL_WINDOWS = (128, 512, 2048)
DIL_RATES = (1, 4, 16)
N_DIL = 3
MEM_TOKENS = 256
MEM_HEADS = 4
D_FF = 5632
NUM_BUCKETS = 32
MAX_DISTANCE = 1024
GRID_W = 64
ROPE_THETA = 10000.0
QBLOCK = 128
EPS = 1e-6
NEG = -1e30

A_Q_W = A_Q_HEADS * HEAD_DIM
A_KV_W = A_KV_HEADS * HEAD_DIM
B_W = B_HEADS * HEAD_DIM
IN_W = A_Q_W + 2 * A_KV_W + 3 * N_DIL * B_W
MIX_W = A_Q_W + B_W
MEM_W = MEM_HEADS * HEAD_DIM

kernel_name = "hybrid_gqa_axial_dilated_memory_macaron"


def rms_norm(x, g):
    xf = x.astype(jnp.float32)
    y = xf * lax.rsqrt(jnp.mean(xf * xf, axis=-1, keepdims=True) + EPS)
    return (y * g.astype(jnp.float32)).astype(x.dtype)


def swiglu(h, w_gate, w_up, w_down):
    return (jax.nn.silu(h @ w_gate) * (h @ w_up)) @ w_down


def axial_rope_tables(seq_len):
    rows = seq_len // GRID_W
    row = jnp.repeat(jnp.arange(rows), GRID_W).astype(jnp.float32)
    col = jnp.tile(jnp.arange(GRID_W), rows).astype(jnp.float32)
    nf = HEAD_DIM // 4
    inv_freq = ROPE_THETA ** (-jnp.arange(nf, dtype=jnp.float32) / nf)
    ang = jnp.stack([row[:, None] * inv_freq, col[:, None] * inv_freq], axis=1)
    return jnp.cos(ang), jnp.sin(ang)


def apply_axial_rope(x, cos, sin):
    nf = HEAD_DIM // 4
    xs = x.astype(jnp.float32).reshape(*x.shape[:-1], 2, 2, nf)
    x1, x2 = xs[..., 0, :], xs[..., 1, :]
    c, s = cos[None, :, None], sin[None, :, None]
    out = jnp.stack([x1 * c - x2 * s, x2 * c + x1 * s], axis=-2)
    return out.reshape(x.shape).astype(x.dtype)


def t5_buckets(rel):
    nb = NUM_BUCKETS // 2
    max_exact = nb // 2
    ret = (rel > 0).astype(np.int32) * nb
    n = np.abs(rel)
    large = max_exact + (np.log(np.maximum(n, 1) / max_exact)
                         / np.log(MAX_DISTANCE / max_exact) * (nb - max_exact)).astype(np.int32)
    large = np.minimum(large, nb - 1)
    return (ret + np.where(n < max_exact, n, large)).astype(np.int32)


def gqa_axial_attention(q, k, v, cos, sin, g_q, g_k):
    bsz, seq_len = q.shape[:2]
    q = apply_axial_rope(rms_norm(q, g_q), cos, sin)
    k = apply_axial_rope(rms_norm(k, g_k), cos, sin)
    grp = A_Q_HEADS // A_KV_HEADS
    nblk = seq_len // QBLOCK
    qb = jnp.moveaxis(q.reshape(bsz, nblk, QBLOCK, A_KV_HEADS, grp, HEAD_DIM), 1, 0)
    scale = HEAD_DIM ** -0.5

    def block(qblk):
        s = jnp.einsum('bqkgd,bskd->bkgqs', qblk, k).astype(jnp.float32) * scale
        p = jax.nn.softmax(s, axis=-1).astype(v.dtype)
        return jnp.einsum('bkgqs,bskd->bqkgd', p, v)

    o = lax.map(block, qb)
    return jnp.moveaxis(o, 0, 1).reshape(bsz, seq_len, A_Q_W)


def dilated_group_attention(q, k, v, bias, dilation, side):
    bsz, seq_len, nh, dh = q.shape
    offsets = dilation * jnp.arange(-side, side + 1)
    nblk = seq_len // QBLOCK
    qb = jnp.moveaxis(q.reshape(bsz, nblk, QBLOCK, nh, dh), 1, 0)
    starts = jnp.arange(nblk) * QBLOCK
    scale = dh ** -0.5

    def block(args):
        qblk, start = args
        idx = start + jnp.arange(QBLOCK)[:, None] + offsets[None, :]
        valid = (idx >= 0) & (idx < seq_len)
        idx = jnp.clip(idx, 0, seq_len - 1)
        kg = k[:, idx]
        vg = v[:, idx]
        s = jnp.einsum('bqhd,bqwhd->bhqw', qblk, kg).astype(jnp.float32) * scale
        s = jnp.where(valid[None, None], s + bias[None, :, None, :], NEG)
        m = jnp.max(s, axis=-1, keepdims=True)
        p = jnp.exp(s - m)
        l = jnp.sum(p, axis=-1)
        o = jnp.einsum('bhqw,bqwhd->bqhd', p.astype(vg.dtype), vg).astype(jnp.float32)
        o = o / jnp.transpose(l, (0, 2, 1))[..., None]
        lse = jnp.transpose(m[..., 0] + jnp.log(l), (0, 2, 1))
        return o, lse

    o, lse = lax.map(block, (qb, starts))
    o = jnp.moveaxis(o, 0, 1).reshape(bsz, seq_len, nh, dh)
    lse = jnp.moveaxis(lse, 0, 1).reshape(bsz, seq_len, nh)
    return o, lse


def dilated_mixture(q, k, v, rel_bias):
    bsz, seq_len = q.shape[:2]
    outs, lses = [], []
    for g in range(N_DIL):
        side = DIL_WINDOWS[g] // (2 * DIL_RATES[g])
        rel = DIL_RATES[g] * np.arange(-side, side + 1)
        buckets = jnp.asarray(t5_buckets(rel))
        bias = rel_bias[:, g * B_HEADS:(g + 1) * B_HEADS].astype(jnp.float32)[buckets].T
        o, lse = dilated_group_attention(q[:, :, g], k[:, :, g], v[:, :, g], bias,
                                         DIL_RATES[g], side)
        outs.append(o)
        lses.append(lse)
    w = jax.nn.softmax(jnp.stack(lses, 0), axis=0)
    o = jnp.sum(w[..., None] * jnp.stack(outs, 0), axis=0)
    return o.reshape(bsz, seq_len, B_W).astype(q.dtype)


def memory_cross_attention(h, hm, w_q, w_kv, w_o):
    bsz, seq_len = h.shape[:2]
    q = (h @ w_q).reshape(bsz, seq_len, MEM_HEADS, HEAD_DIM)
    kv = hm @ w_kv
    k = kv[..., :MEM_W].reshape(bsz, -1, MEM_HEADS, HEAD_DIM)
    v = kv[..., MEM_W:].reshape(bsz, -1, MEM_HEADS, HEAD_DIM)
    s = jnp.einsum('bshd,bmhd->bhsm', q, k).astype(jnp.float32) * HEAD_DIM ** -0.5
    p = jax.nn.softmax(s, axis=-1).astype(v.dtype)
    o = jnp.einsum('bhsm,bmhd->bshd', p, v).reshape(bsz, seq_len, MEM_W)
    return o @ w_o


def setup_inputs(seed: int = 0) -> dict:
    key = jax.random.key(seed)
    ks = jax.random.split(key, 24)
    f32 = jnp.float32

    def dense(k, shape, fan_in):
        return jax.random.normal(k, shape, f32) * (fan_in ** -0.5)

    def gain(k, shape):
        return 1.0 + 0.05 * jax.random.normal(k, shape, f32)

    L = DEPTH
    return {
        "x": jax.random.normal(ks[0], (BATCH, SEQ, D_MODEL), f32),
        "mem": jax.random.normal(ks[1], (BATCH, MEM_TOKENS, D_MODEL), f32),
        "ffn1_norm": gain(ks[2], (L, D_MODEL)),
        "ffn1_w_gate": dense(ks[3], (L, D_MODEL, D_FF), D_MODEL),
        "ffn1_w_up": dense(ks[4], (L, D_MODEL, D_FF), D_MODEL),
        "ffn1_w_down": dense(ks[5], (L, D_FF, D_MODEL), D_FF),
        "mix_norm": gain(ks[6], (L, D_MODEL)),
        "w_in": dense(ks[7], (L, D_MODEL, IN_W), D_MODEL),
        "q_norm_a": gain(ks[8], (L, HEAD_DIM)),
        "k_norm_a": gain(ks[9], (L, HEAD_DIM)),
        "rel_bias": 0.1 * jax.random.normal(ks[10], (NUM_BUCKETS, N_DIL * B_HEADS), f32),
        "w_out": dense(ks[11], (L, MIX_W, D_MODEL), MIX_W),
        "mem_x_norm": gain(ks[12], (L, D_MODEL)),
        "mem_m_norm": gain(ks[13], (L, D_MODEL)),
        "w_q_mem": dense(ks[14], (L, D_MODEL, MEM_W), D_MODEL),
        "w_kv_mem": dense(ks[15], (L, D_MODEL, 2 * MEM_W), D_MODEL),
        "w_o_mem": dense(ks[16], (L, MEM_W, D_MODEL), MEM_W),
        "ffn2_norm": gain(ks[17], (L, D_MODEL)),
        "ffn2_w_gate": dense(ks[18], (L, D_MODEL, D_FF), D_MODEL),
        "ffn2_w_up": dense(ks[19], (L, D_MODEL, D_FF), D_MODEL),
        "ffn2_w_down": dense(ks[20], (L, D_FF, D_MODEL), D_FF),
        "final_norm": gain(ks[21], (D_MODEL,)),
    }


def reference(x, mem, ffn1_norm, ffn1_w_gate, ffn1_w_up, ffn1_w_down, mix_norm, w_in,
              q_norm_a, k_norm_a, rel_bias, w_out, mem_x_norm, mem_m_norm, w_q_mem,
              w_kv_mem, w_o_mem, ffn2_norm, ffn2_w_gate, ffn2_w_up, ffn2_w_down, final_norm):
    bsz, seq_len, _ = x.shape
    cos, sin = axial_rope_tables(seq_len)
    o_aq = A_Q_W
    o_ak = o_aq + A_KV_W
    o_av = o_ak + A_KV_W
    o_bq = o_av + N_DIL * B_W
    o_bk = o_bq + N_DIL * B_W
    for l in range(DEPTH):
        x = x + 0.5 * swiglu(rms_norm(x, ffn1_norm[l]), ffn1_w_gate[l], ffn1_w_up[l], ffn1_w_down[l])

        h = rms_norm(x, mix_norm[l])
        proj = h @ w_in[l]
        qa = proj[..., :o_aq].reshape(bsz, seq_len, A_Q_HEADS, HEAD_DIM)
        ka = proj[..., o_aq:o_ak].reshape(bsz, seq_len, A_KV_HEADS, HEAD_DIM)
        va = proj[..., o_ak:o_av].reshape(bsz, seq_len, A_KV_HEADS, HEAD_DIM)
        qb = proj[..., o_av:o_bq].reshape(bsz, seq_len, N_DIL, B_HEADS, HEAD_DIM)
        kb = proj[..., o_bq:o_bk].reshape(bsz, seq_len, N_DIL, B_HEADS, HEAD_DIM)
        vb = proj[..., o_bk:].reshape(bsz, seq_len, N_DIL, B_HEADS, HEAD_DIM)
        out_a = gqa_axial_attention(qa, ka, va, cos, sin, q_norm_a[l], k_norm_a[l])
        out_b = dilated_mixture(qb, kb, vb, rel_bias)
        x = x + jnp.concatenate([out_a, out_b], axis=-1) @ w_out[l]

        x = x + memory_cross_attention(rms_norm(x, mem_x_norm[l]), rms_norm(mem, mem_m_norm[l]),
                                       w_q_mem[l], w_kv_mem[l], w_o_mem[l])

        x = x + 0.5 * swiglu(rms_norm(x, ffn2_norm[l]), ffn2_w_gate[l], ffn2_w_up[l], ffn2_w_down[l])
    return rms_norm(x, final_norm)
```

```python
import numpy as np
from contextlib import ExitStack
import concourse.bass as bass
import concourse.mybir as mybir
from concourse.bass_utils import run_bass_kernel_spmd

F32 = mybir.dt.float32
BF16 = mybir.dt.bfloat16
ALU = mybir.AluOpType
AF = mybir.ActivationFunctionType

D = 2048
KC = 16
DFF = 5632
FC = 44
TT = 512
SEQ = 16384
OWN = 4096
HALO = 1024
IN_W = 10752
EPS = 1e-6
SCALE = 128 ** -0.5
NEGM = -3000.0
RATES = (1, 4, 16)
NBLK = (32, 8, 2)
CH_OFF = (0, 33, 33 + 36)
NCH = (33, 36, 48)


class Prog:
    CE = ("pe", "act", "dve", "pool")
    QE = ("sp", "act", "pool")
    NPOOL = 24

    def __init__(self):
        self.ops = []
        self.lw = {}
        self.rd_c = {}
        self.rd_d = {}
        self.last_c = {}
        self.dma_all = []
        self.log = None

    def add(self, eng, fn, reads=(), writes=(), dma=False):
        i = len(self.ops)
        deps = {}
        for k in reads:
            w = self.lw.get(k)
            if w is not None:
                deps[w] = "raw"
            if k[0] == "ps":
                for e2, r in self.rd_c.get(k, {}).items():
                    if e2 != eng:
                        deps.setdefault(r, "war")
        for k in writes:
            w = self.lw.get(k)
            if w is not None:
                deps.setdefault(w, "waw")
            for r in self.rd_c.get(k, {}).values():
                deps.setdefault(r, "war")
            for r in self.rd_d.get(k, ()):
                deps.setdefault(r, "war")
        deps.pop(i, None)
        self.ops.append(dict(eng=eng, fn=fn, dma=dma, deps=deps))
        for k in reads:
            if dma:
                self.rd_d.setdefault(k, []).append(i)
            else:
                self.rd_c.setdefault(k, {})[eng] = i
        for k in writes:
            self.lw[k] = i
            self.rd_c[k] = {}
            self.rd_d[k] = []
        if dma:
            self.dma_all.append(i)
        else:
            self.last_c[eng] = i
        return i

    def barrier(self):
        prev_c = dict(self.last_c)
        prev_d = list(self.dma_all)
        for e in ("pe", "act", "dve", "pool", "sp"):
            i = len(self.ops)
            deps = {j: "raw" for j in prev_c.values()}
            for j in prev_d:
                deps[j] = "raw"
            self.ops.append(dict(eng=e, fn=None, dma=False, deps=deps))
        self.lw = {}
        self.rd_c = {}
        self.rd_d = {}

    def emit(self, nc, st):
        ops = self.ops
        qcount = {q: 0 for q in self.QE}
        for o in ops:
            if o["dma"]:
                q = o["eng"]
                k = qcount[q]
                qcount[q] += 1
                o["dk"] = k
        esem = {e: st.enter_context(nc.semaphore("se_" + e)) for e in self.CE}
        qsem = {q: [st.enter_context(nc.semaphore("sq_%s_%d" % (q, j))) for j in range(self.NPOOL)]
                for q in self.QE}
        P = self.NPOOL
        need = [False] * len(ops)
        for i, o in enumerate(ops):
            kept = {}
            best_same = {}
            for j, kind in o["deps"].items():
                oj = ops[j]
                if oj["dma"]:
                    kept[j] = kind
                    continue
                if oj["fn"] is None and oj["eng"] == "sp":
                    continue
                if (not o["dma"]) and oj["eng"] == o["eng"]:
                    if o["eng"] == "pe" or o["fn"] is None:
                        continue
                e = oj["eng"]
                if e not in best_same or best_same[e] < j:
                    best_same[e] = j
            for e, j in best_same.items():
                kept[j] = "x"
            o["kept"] = kept
            for j in kept:
                if not ops[j]["dma"]:
                    need[j] = True
        ms = {}
        cnt = {e: 0 for e in self.CE}
        for i, o in enumerate(ops):
            if not o["dma"] and need[i] and o["eng"] in cnt:
                cnt[o["eng"]] += 1
                ms[i] = cnt[o["eng"]]
        self.stats = dict(cnt)

        def semval(j):
            oj = ops[j]
            if oj["dma"]:
                k = oj["dk"]
                return qsem[oj["eng"]][k % P], 16 * (k // P + 1)
            return esem[oj["eng"]], ms[j]

        def run(engname, eng):
            waited = {}
            for i, o in enumerate(ops):
                if o["eng"] != engname:
                    continue
                ws = []
                for j in o["kept"]:
                    ws.append(semval(j))
                if o["dma"]:
                    k = o["dk"]
                    if k >= P:
                        ws.append((qsem[engname][k % P], 16 * (k // P)))
                for s, v in ws:
                    key = id(s)
                    if waited.get(key, 0) >= v:
                        continue
                    waited[key] = v
                    eng.wait_ge(s, v)
                    if self.log is not None:
                        self.log.append((engname, i, "wait", s.name, v))
                if self.log is not None:
                    self.log.append((engname, i, "op", o.get("dk"), ms.get(i)))
                if o["fn"] is None:
                    if i in ms:
                        eng.nop().then_inc(esem[engname], 1)
                    continue
                ins = o["fn"](eng)
                if o["dma"]:
                    k = o["dk"]
                    ins.then_inc(qsem[engname][k % P], 16)
                elif i in ms:
                    ins.then_inc(esem[engname], 1)

        with nc.Block() as block:
            @block.tensor
            def _(e):
                run("pe", e)

            @block.scalar
            def _(e):
                run("act", e)

            @block.vector
            def _(e):
                run("dve", e)

            @block.gpsimd
            def _(e):
                run("pool", e)

            @block.sync
            def _(e):
                run("sp", e)


class Arena:
    def __init__(self, t, nwords):
        self.t = t
        self.n = nwords
        self.off = 0

    def f32(self, n):
        assert self.off + n <= self.n, ("arena overflow", self.off, n, self.n)
        ap = self.t[:, self.off:self.off + n]
        self.off += n
        return ap

    def bf(self, n):
        w = (n + 1) // 2
        assert self.off + w <= self.n, ("arena overflow", self.off, w, self.n)
        ap = self.t[:, self.off:self.off + w].bitcast(BF16)
        self.off += w
        return ap


ARENA_WORDS = 46 * 1024


def build(cfg=None):
    cfg = cfg or {}
    NT_ALL = cfg.get("nt_all", SEQ // TT)
    NT_OWN = cfg.get("nt_own", OWN // TT)
    DBG = cfg.get("dbg", False)
    PH = cfg.get("phases", "0ABDC")
    nc = bass.Bass("TRN2", target_bir_lowering=False)
    st = ExitStack()
    P = Prog()

    def din(name, shape, dt=F32):
        return nc.dram_tensor(name, list(shape), dt, kind="ExternalInput").ap()

    DUMP = cfg.get("dump", ())

    def dscr(name, shape, dt):
        kind = "ExternalOutput" if name in DUMP else ("ExternalInput" if name in cfg.get("ext_in", ()) else "Internal")
        return nc.dram_tensor(name, list(shape), dt, kind=kind).ap()

    xT = din("xT", [D, SEQ])
    memT = din("memT", [D, 256])
    w_in32 = {
        "g1": din("ffn1_w_gate", [D, DFF]), "u1": din("ffn1_w_up", [D, DFF]), "d1": din("ffn1_w_down", [DFF, D]),
        "win": din("w_in", [D, IN_W]), "wout": din("w_out", [D, D]),
        "wqm": din("w_q_mem", [D, 512]), "wkvm": din("w_kv_mem", [D, 1024]), "wom": din("w_o_mem", [512, D]),
        "g2": din("ffn2_w_gate", [D, DFF]), "u2": din("ffn2_w_up", [D, DFF]), "d2": din("ffn2_w_down", [DFF, D]),
    }
    gains_in = din("gains", [128, 6 * KC])
    qkg_in = din("qkg", [128, 2])
    cosT = din("cosT", [128, SEQ])
    sinT = din("sinT", [128, SEQ])
    cmat_in = din("cmat", [128, 3 * 128])
    mrows_in = din("mrows", [1, 256])
    relb_in = din("rel_bias", [32, 24])
    oh_in = din("ohaug", [33, 3 * 384])
    outT = nc.dram_tensor("outT", [D, OWN], F32, kind="ExternalOutput").ap()

    wbf = {k: dscr("bf_" + k, v.shape, BF16) for k, v in w_in32.items()}
    x1T = dscr("x1T", [D, OWN], F32)
    KaT = dscr("KaT", [2, 128, SEQ], BF16)
    Va = dscr("Va", [SEQ, 256], BF16)
    QaT = dscr("QaT", [8, 128, OWN], BF16)
    QbT = dscr("QbT", [24, 128, OWN], BF16)
    KbT = dscr("KbT", [24, 128, OWN + 2 * HALO], BF16)
    Vb = dscr("Vb", [OWN + 2 * HALO, 3072], BF16)
    mixT = dscr("mixT", [D, OWN], BF16)
    fbuf = dscr("fbuf", [24, 384], BF16)

    arena_t = st.enter_context(nc.sbuf_tensor("arena", [128, ARENA_WORDS], F32))
    ones_bf = st.enter_context(nc.sbuf_tensor("ones_bf", [128, 128], BF16))
    cmat = st.enter_context(nc.sbuf_tensor("cmat_sb", [128, 3 * 128], BF16))
    gains = st.enter_context(nc.sbuf_tensor("gains_sb", [128, 6 * KC], F32))
    qkg = st.enter_context(nc.sbuf_tensor("qkg_sb", [128, 2], F32))
    mrows = st.enter_context(nc.sbuf_tensor("mrows_sb", [1, 256], BF16))
    ones_row = st.enter_context(nc.sbuf_tensor("ones_row", [1, 128], BF16))
    relb = st.enter_context(nc.sbuf_tensor("relb_sb", [33, 24], F32))
    ohaug = st.enter_context(nc.sbuf_tensor("ohaug_sb", [33, 3 * 384], F32))
    ps = [st.enter_context(nc.psum_tensor("ps%d" % i, [128, 512], F32)) for i in range(8)]
    ident = cmat[:, 0:128]
    ropeT = cmat[:, 128:256]
    antiI = cmat[:, 256:384]
    A = Arena(arena_t, ARENA_WORDS)

    psn = [0]

    def nextbank():
        b = psn[0] % 8
        psn[0] += 1
        return b

    STQ = cfg.get("stq", "pool")

    def dma(q, out, in_, reads, writes):
        if q == "pool" and reads:
            q = STQ
        return P.add(q, lambda e, o=out, i=in_: e.dma_start(out=o, in_=i), reads=reads, writes=writes, dma=True)

    open_grp = {}

    def mm(out, lhsT, rhs, start, stop, reads, writes, gkey=None, first=None, skip=False):
        if first is None:
            first = start
        i = P.add("pe", lambda e, o=out, l=lhsT, r=rhs, s0=start, s1=stop, sk=skip: e.matmul(
            o, l, r, start=s0, stop=s1, skip_group_check=sk), reads=reads, writes=writes)
        gk = gkey if gkey is not None else tuple(writes)
        start = first
        if start:
            open_grp[gk] = []
        open_grp[gk].append(i)
        if stop:
            for j in open_grp.pop(gk):
                P.ops[j]["redir"] = i
        return i

    def act(out, in_, func, reads, writes, scale=1.0, bias=0.0):
        return P.add("act", lambda e, o=out, i=in_, f=func, s=scale, b=bias: e.activation(o, i, f, bias=b, scale=s),
                     reads=reads, writes=writes)

    def ts(eng, out, in0, s1, s2, op0, op1, reads, writes):
        if op1 is None:
            return P.add(eng, lambda e, o=out, i=in0, a=s1, p0=op0: e.tensor_scalar(o, i, a, None, p0),
                         reads=reads, writes=writes)
        return P.add(eng, lambda e, o=out, i=in0, a=s1, b=s2, p0=op0, p1=op1: e.tensor_scalar(o, i, a, b, p0, p1),
                     reads=reads, writes=writes)

    def stt(eng, out, in0, sc, in1, op0, op1, reads, writes):
        return P.add(eng, lambda e, o=out, i=in0, s=sc, j=in1, p0=op0, p1=op1: e.scalar_tensor_tensor(o, i, s, j, p0, p1),
                     reads=reads, writes=writes)

    def tt(eng, out, in0, in1, op, reads, writes):
        return P.add(eng, lambda e, o=out, i=in0, j=in1, p=op: e.tensor_tensor(o, i, j, p), reads=reads, writes=writes)

    def cp(eng, out, in_, reads, writes):
        return P.add(eng, lambda e, o=out, i=in_: e.tensor_copy(o, i), reads=reads, writes=writes)

    def transp(out, in_, reads, writes):
        return P.add("pe", lambda e, o=out, i=in_: e.transpose(o, i, ident), reads=reads + [("c", "cmat")], writes=writes)

    def recip(eng, out, in_, reads, writes):
        return P.add(eng, lambda e, o=out, i=in_: e.reciprocal(o, i), reads=reads, writes=writes)

    def memset(eng, ap, val, writes):
        return P.add(eng, lambda e, a=ap, v=val: e.memset(a, v), reads=(), writes=writes)

    memset("pool", ones_bf[:, :], 1.0, [("c", "ones")])
    memset("pool", ones_row[:, :], 1.0, [("c", "onesrow")])
    dma("pool", cmat[:, :], cmat_in, [], [("c", "cmat")])
    dma("pool", mrows[:, :], mrows_in, [], [("c", "mrows")])
    dma("sp", gains[:, :], gains_in, [], [("c", "gains")])
    dma("sp", qkg[:, :], qkg_in, [], [("c", "qkg")])
    dma("sp", relb[0:32, :], relb_in, [], [("c", "relb")])
    memset("pool", relb[32:33, :], 1.0, [("c", "relb1")])
    dma("sp", ohaug[:, :], oh_in, [], [("c", "ohaug")])

    def wkeys(name, r0, r1):
        return [("w", name, rc) for rc in range(r0 // 128, (r1 + 127) // 128)]

    def convert(name):
        src, dst = w_in32[name], wbf[name]
        rows, cols = src.shape
        rb = 256 if cols > 4096 else 512
        rb = min(rb, rows)
        for r0 in range(0, rows, rb):
            dma("pool", dst[r0:r0 + rb, :], src[r0:r0 + rb, :], [], wkeys(name, r0, r0 + rb))

    if "0" in PH:
        for name in ("g1", "u1", "d1", "win", "wout", "wqm", "wkvm", "wom", "g2", "u2", "d2"):
            convert(name)

    def phase_ffn_buffers():
        A.off = 0
        B_ = {}
        B_["xs"] = A.f32(KC * TT).rearrange("p (k t) -> p k t", t=TT)
        B_["hid"] = A.bf(FC * TT).rearrange("p (k t) -> p k t", t=TT)
        B_["hs"] = A.bf(KC * TT).rearrange("p (k t) -> p k t", t=TT)
        B_["ws"] = [A.bf(8192) for _ in range(3 if cfg.get("sqsep") else 4)]
        if cfg.get("sqsep"):
            B_["sq"] = A.bf(KC * TT).rearrange("p (k t) -> p k t", t=TT)
        B_["rstd"] = A.f32(TT)
        B_["sg"] = [A.f32(TT) for _ in range(2)]
        return B_

    wsn = [0]

    def load_w(B_, name, r0, r1, c0, c1):
        s = wsn[0] % len(B_["ws"])
        wsn[0] += 1
        nk = (r1 - r0) // 128
        ncol = c1 - c0
        dst = B_["ws"][s][:, 0:nk * ncol].rearrange("p (k c) -> p k c", c=ncol)
        src = wbf[name][r0:r1, c0:c1].rearrange("(k p) c -> p k c", p=128)
        dma("sp", dst, src, wkeys(name, r0, r1), [("ws", s)])
        return s, dst

    def rmsnorm(B_, gidx, ntok=TT, fp32_out=None):
        xs, hid, hs, rstd = B_["xs"], B_["hid"], B_["hs"], B_["rstd"]
        NS = cfg.get("nstep", 99) if gidx == 1 else 99
        hkey = "hid"
        if "sq" in B_:
            hid = B_["sq"]
            hkey = "sq"
        if NS < 1:
            return
        for q in range(4):
            tt(cfg.get("sqeng", "pool"), hid[:, q * 4:(q + 1) * 4, 0:ntok], xs[:, q * 4:(q + 1) * 4, 0:ntok], xs[:, q * 4:(q + 1) * 4, 0:ntok],
               ALU.mult, [("xs", kc) for kc in range(q * 4, q * 4 + 4)], [(hkey, kc) for kc in range(q * 4, q * 4 + 4)])
        if NS < 2:
            return
        b = nextbank()
        for kc in range(KC):
            mm(ps[b][:, 0:ntok], ones_bf[:, :], hid[:, kc, 0:ntok], kc == 0, kc == KC - 1,
               [(hkey, kc), ("c", "ones")], [("ps", b)])
        if NS < 3:
            return
        act(rstd[:, 0:ntok], ps[b][:, 0:ntok], AF.Sqrt, [("ps", b)], [("rstd",)], scale=1.0 / D, bias=EPS)
        if NS < 4:
            return
        recip("dve", rstd[:, 0:ntok], rstd[:, 0:ntok], [("rstd",)], [("rstd",)])
        if NS < 5:
            return
        for kc in range(KC):
            if fp32_out is None:
                stt("dve", hs[:, kc, 0:ntok], xs[:, kc, 0:ntok], gains[:, gidx * KC + kc:gidx * KC + kc + 1], rstd[:, 0:ntok],
                    ALU.mult, ALU.mult, [("xs", kc), ("rstd",), ("c", "gains")], [("hs", kc)])
            else:
                stt("dve", xs[:, kc, 0:ntok], xs[:, kc, 0:ntok], gains[:, gidx * KC + kc:gidx * KC + kc + 1], rstd[:, 0:ntok],
                    ALU.mult, ALU.mult, [("xs", kc), ("rstd",), ("c", "gains")], [("xs", kc)])

    def ffn(B_, wg, wu, wd):
        xs, hid, hs, sgt = B_["xs"], B_["hid"], B_["hs"], B_["sg"]
        for J in range(FC // 4):
            s_g, wgv = load_w(B_, wg, 0, D, J * 512, (J + 1) * 512)
            s_u, wuv = load_w(B_, wu, 0, D, J * 512, (J + 1) * 512)
            for jj in range(4):
                j = J * 4 + jj
                bg = nextbank()
                bu = nextbank()
                for kc in range(KC):
                    mm(ps[bg][:, :], wgv[:, kc, jj * 128:(jj + 1) * 128], hs[:, kc, :], kc == 0, kc == KC - 1,
                       [("ws", s_g), ("hs", kc)], [("ps", bg)])
                for kc in range(KC):
                    mm(ps[bu][:, :], wuv[:, kc, jj * 128:(jj + 1) * 128], hs[:, kc, :], kc == 0, kc == KC - 1,
                       [("ws", s_u), ("hs", kc)], [("ps", bu)])
                si = j % 2
                act(sgt[si], ps[bg][:, :], AF.Silu, [("ps", bg)], [("sg", si)])
                tt("dve", hid[:, j, :], sgt[si], ps[bu][:, :], ALU.mult, [("sg", si), ("ps", bu)], [("hid", j)])
        for N in range(4):
            banks = [nextbank() for _ in range(4)]
            for q in range(4):
                s_d, wdv = load_w(B_, wd, q * 11 * 128, (q + 1) * 11 * 128, N * 512, (N + 1) * 512)
                for kl in range(11):
                    k = q * 11 + kl
                    for n4 in range(4):
                        mm(ps[banks[n4]][:, :], wdv[:, kl, n4 * 128:(n4 + 1) * 128], hid[:, k, :], k == 0, k == FC - 1,
                           [("ws", s_d), ("hid", k)], [("ps", banks[n4])])
            for n4 in range(4):
                c = N * 4 + n4
                stt("dve", xs[:, c, :], ps[banks[n4]][:, :], 0.5, xs[:, c, :], ALU.mult, ALU.add,
                    [("ps", banks[n4]), ("xs", c)], [("xs", c)])

    if "A" in PH:
        B_ = phase_ffn_buffers()
        xs, hs = B_["xs"], B_["hs"]
        ctab = A.f32(TT)
        stab = A.f32(TT)
        t1 = A.f32(TT)
        t2 = A.f32(TT)
        rs2 = A.f32(TT)
        kgb = A.bf(TT)
        sqk = A.bf(TT)
        kr = [A.bf(TT) for _ in range(2)]
        stg = [A.bf(TT) for _ in range(3)]
        vst = [A.bf(4 * 256).rearrange("p (a c) -> p a c", c=256) for _ in range(2)]
        cnt = {"kr": 0, "stg": 0, "vst": 0, "cp": 0}

        def normrope(b, gi, dst):
            NR = cfg.get("nr", 99)
            if NR < 1:
                return
            act(sqk, ps[b][:, :], AF.Square, [("ps", b)], [("sqk",)])
            if NR < 2:
                return
            ts("dve", kgb, ps[b][:, :], qkg[:, gi:gi + 1], None, ALU.mult, None, [("ps", b), ("c", "qkg")], [("kgb",)])
            if NR < 3:
                return
            b2 = nextbank()
            mm(ps[b2][:, :], ones_bf[:, :], sqk, True, True, [("sqk",), ("c", "ones")], [("ps", b2)])
            if NR < 4:
                return
            b3 = nextbank()
            mm(ps[b3][:, :], ropeT, kgb, True, True, [("kgb",), ("c", "cmat")], [("ps", b3)])
            if NR < 5:
                return
            VAR = cfg.get("var", 0)
            if VAR == 1:
                act(rs2, ps[b2][:, :], AF.Copy, [("ps", b2)], [("rs2",)])
            elif VAR == 2:
                act(rs2, ps[b2][:, :], AF.Sqrt, [("ps", b2)], [("rs2",)], scale=1.0 / 128, bias=EPS)
                return
            elif VAR == 3:
                act(rs2, ps[b2][:, :], AF.Sqrt, [("ps", b2)], [("rs2",)], scale=1.0 / D, bias=EPS)
            else:
                act(rs2, ps[b2][:, :], AF.Sqrt, [("ps", b2)], [("rs2",)], scale=1.0 / 128, bias=EPS)
            recip("dve", rs2, rs2, [("rs2",)], [("rs2",)])
            if NR < 6:
                return
            stt("dve", t1, ps[b][:, :], qkg[:, gi:gi + 1], ctab, ALU.mult, ALU.mult, [("ps", b), ("ctab",), ("c", "qkg")], [("t1",)])
            tt("dve", t2, ps[b3][:, :], stab, ALU.mult, [("ps", b3), ("stab",)], [("t2",)])
            tt("dve", t1, t1, t2, ALU.add, [("t1",), ("t2",)], [("t1",)])
            if NR < 7:
                return
            si = cnt["kr"] % 2
            cnt["kr"] += 1
            tt("dve", kr[si], t1, rs2, ALU.mult, [("t1",), ("rs2",)], [("kr", si)])
            if NR < 8:
                return
            dma("pool", dst, kr[si], [("kr", si)], [])

        def evac_copy(out, in_, reads, writes):
            cnt["cp"] += 1
            if cnt["cp"] % 2:
                act(out, in_, AF.Copy, reads, writes)
            else:
                cp("dve", out, in_, reads, writes)

        def proj_fm_heads(c0, nheads, dst_fn):
            for h0 in range(0, nheads, 4):
                s, wv = load_w(B_, "win", 0, D, c0 + h0 * 128, c0 + (h0 + 4) * 128)
                for hh in range(4):
                    b = nextbank()
                    for kc in range(KC):
                        mm(ps[b][:, :], wv[:, kc, hh * 128:(hh + 1) * 128], hs[:, kc, :], kc == 0, kc == KC - 1,
                           [("ws", s), ("hs", kc)], [("ps", b)])
                    dst_fn(h0 + hh, b)

        def plain_store(dst):
            def f(b):
                si = cnt["stg"] % 3
                cnt["stg"] += 1
                evac_copy(stg[si], ps[b][:, :], [("ps", b)], [("stg", si)])
                dma("pool", dst, stg[si], [("stg", si)], [])
            return f

        for ti in list(range(NT_ALL)):
            p0 = ti * TT
            own = ti < NT_OWN
            halo = (NT_OWN <= ti < NT_OWN + 2) or ti >= NT_ALL - 2
            pos0 = p0 if ti < NT_OWN + 2 else p0 - NT_ALL * TT
            dma("sp", xs, xT[:, p0:p0 + TT].rearrange("(k p) t -> p k t", p=128), [], [("xs", kc) for kc in range(KC)])
            STOP = cfg.get("stop", 99)
            if STOP < 1:
                continue
            rmsnorm(B_, 0)
            if STOP < 2:
                continue
            ffn(B_, "g1", "u1", "d1")
            if STOP < 3:
                continue
            if own:
                dma("pool", x1T[:, p0:p0 + TT].rearrange("(k p) t -> p k t", p=128), xs, [("xs", kc) for kc in range(KC)], [])
            if STOP < 3.5:
                continue
            rmsnorm(B_, 1)
            if STOP < 4:
                continue
            dma("sp", ctab, cosT[:, p0:p0 + TT], [], [("ctab",)])
            dma("sp", stab, sinT[:, p0:p0 + TT], [], [("stab",)])
            s, wv = load_w(B_, "win", 0, D, 1024, 1536)
            for hd in range(2):
                b = nextbank()
                for kc in range(KC):
                    mm(ps[b][:, :], wv[:, kc, hd * 128:(hd + 1) * 128], hs[:, kc, :], kc == 0, kc == KC - 1,
                       [("ws", s), ("hs", kc)], [("ps", b)])
                normrope(b, 1, KaT[hd, :, p0:p0 + TT])
            if STOP < 4.1:
                continue
            vi = cnt["vst"] % 2
            cnt["vst"] += 1
            for tb in range(4):
                b = nextbank()
                for kc in range(KC):
                    mm(ps[b][:, 0:256], hs[:, kc, tb * 128:(tb + 1) * 128], wv[:, kc, 256:512], kc == 0, kc == KC - 1,
                       [("ws", s), ("hs", kc)], [("ps", b)])
                evac_copy(vst[vi][:, tb, :], ps[b][:, 0:256], [("ps", b)], [("vst", vi)])
            dma("pool", Va[p0:p0 + TT, :].rearrange("(a p) c -> p a c", p=128), vst[vi], [("vst", vi)], [])
            if STOP < 4.2:
                continue
            if own:
                proj_fm_heads(0, 8, lambda h, b: normrope(b, 0, QaT[h, :, p0:p0 + TT]))
                if STOP < 4.3:
                    continue
                proj_fm_heads(1536, 24, lambda h, b: plain_store(QbT[h, :, p0:p0 + TT])(b))
            if STOP < 4.4:
                continue
            if own or halo:
                c0 = pos0 + HALO
                proj_fm_heads(4608, 24, lambda h, b: plain_store(KbT[h, :, c0:c0 + TT])(b))
                if STOP < 4.5:
                    continue
                for i6 in range(6):
                    s, wv = load_w(B_, "win", 0, D, 7680 + i6 * 512, 7680 + (i6 + 1) * 512)
                    for tb in range(4):
                        b = nextbank()
                        for kc in range(KC):
                            mm(ps[b][:, :], hs[:, kc, tb * 128:(tb + 1) * 128], wv[:, kc, :], kc == 0, kc == KC - 1,
                               [("ws", s), ("hs", kc)], [("ps", b)])
                        plain_store(Vb[c0 + tb * 128:c0 + (tb + 1) * 128, i6 * 512:(i6 + 1) * 512])(b)
        P.barrier()


    OWN_T = NT_OWN * TT
    if "B" in PH:
        A.off = 0
        NKC = NT_ALL * 4
        NQB = NT_OWN * 4
        SK = NT_ALL * TT
        Kt = [A.bf(SK) for _ in range(2)]
        Vt = [A.bf(NKC * 129).rearrange("p (c d) -> p c d", d=129) for _ in range(2)]
        Qt = [A.bf(512) for _ in range(2)]
        PT = [A.bf(512) for _ in range(4)]
        osb = [A.bf(512).rearrange("p (h d) -> p h d", d=128) for _ in range(2)]
        rl = [A.f32(4) for _ in range(2)]
        mst = [A.bf(512) for _ in range(2)]
        KPC = 8
        for g in range(2):
            memset("pool", Vt[g][:, :, 128:129], 1.0, [("V1", g)])
            for pc in range(NKC // KPC):
                c0 = pc * KPC
                dma("sp", Kt[g][:, c0 * 128:(c0 + KPC) * 128], KaT[g, :, c0 * 128:(c0 + KPC) * 128], [], [("K", g, pc)])
                dma("sp", Vt[g][:, c0:c0 + KPC, 0:128],
                    Va[c0 * 128:(c0 + KPC) * 128, g * 128:(g + 1) * 128].rearrange("(c p) d -> p c d", p=128),
                    [], [("V", g, pc)])
        its = [(g, qb, kc) for g in range(2) for qb in range(NQB) for kc in range(NKC)]
        sbank = [0]

        def issue_S(idx):
            g, qb, kc = its[idx]
            blk = g * NQB + qb
            qs = blk % 2
            if kc == 0:
                dma("sp", Qt[qs].rearrange("p (h q) -> p h q", q=128),
                    QaT[4 * g:4 * g + 4, :, qb * 128:(qb + 1) * 128].rearrange("h d q -> d h q"), [], [("Qt", qs)])
            b = sbank[0] % 4
            sbank[0] += 1
            mm(ps[b][:, :], Kt[g][:, kc * 128:(kc + 1) * 128], Qt[qs], True, True,
               [("K", g, kc // KPC), ("Qt", qs)], [("ps", b)])
            pi = idx % 4
            act(PT[pi], ps[b][:, :], AF.Exp, [("ps", b)], [("PT", pi)], scale=SCALE)

        def issue_PV(idx):
            g, qb, kc = its[idx]
            blk = g * NQB + qb
            par = blk % 2
            pi = idx % 4
            for hh in range(4):
                bank = 4 + 2 * par + hh // 2
                c0 = (hh % 2) * 256
                mm(ps[bank][:, c0:c0 + 129], PT[pi][:, hh * 128:(hh + 1) * 128], Vt[g][:, kc, :],
                   (kc == 0 and hh % 2 == 0), kc == NKC - 1,
                   [("PT", pi), ("V", g, kc // KPC), ("V1", g)], [("ps", bank)], gkey=("acc", par, hh), first=(kc == 0), skip=True)
            if kc == NKC - 1:
                for hh in range(4):
                    bank = 4 + 2 * par + hh // 2
                    c0 = (hh % 2) * 256
                    recip("dve", rl[par][:, hh:hh + 1], ps[bank][:, c0 + 128:c0 + 129], [("ps", bank)], [("rl", par, hh)])
                    ts("dve", osb[par][:, hh, :], ps[bank][:, c0:c0 + 128], rl[par][:, hh:hh + 1], None, ALU.mult, None,
                       [("ps", bank), ("rl", par, hh)], [("osb", par, hh)])
                b = sbank[0] % 4
                sbank[0] += 1
                pst = ps[b].bitcast(BF16)
                for hh in range(4):
                    transp(pst[:, hh * 128:(hh + 1) * 128], osb[par][:, hh, :], [("osb", par, hh)], [("ps", b)])
                cp("dve", mst[par], pst[:, 0:512], [("ps", b)], [("mst", par)])
                dma("pool", mixT[4 * g * 128:(4 * g + 4) * 128, qb * 128:(qb + 1) * 128].rearrange("(h d) q -> d h q", d=128),
                    mst[par].rearrange("p (h q) -> p h q", q=128), [("mst", par)], [])

        LA = 2
        for idx in range(min(LA, len(its))):
            issue_S(idx)
        for idx in range(len(its)):
            if idx + LA < len(its):
                issue_S(idx + LA)
            issue_PV(idx)
        P.barrier()

    if "D" in PH:
        A.off = 0
        KW = OWN_T + 2 * HALO
        nblk = [OWN_T // (128 * r) for r in RATES]
        fb_sb = A.bf(384)
        for g in range(3):
            b = nextbank()
            mm(ps[b][0:8, 0:384], relb[0:33, g * 8:(g + 1) * 8], ohaug[0:33, g * 384:(g + 1) * 384], True, True,
               [("c", "relb"), ("c", "relb1"), ("c", "ohaug")], [("ps", b)])
            act(fb_sb[0:8, :], ps[b][0:8, 0:384], AF.Copy, [("ps", b)], [("fb",)], scale=1.0 / SCALE)
            dma("pool", fbuf[g * 8:(g + 1) * 8, :], fb_sb[0:8, :], [("fb",)], [("fbuf", g)])
        Hsb = A.bf(24 * 2 * 128).rearrange("p (a c k) -> p a c k", c=2, k=128)
        fb_t = fbuf.tensor
        for gh in range(24):
            for cc in range(2):
                src = bass.AP(tensor=fb_t, offset=gh * 384 + 128 * cc, ap=[[1, 128], [1, 128]])
                dma("sp", Hsb[:, gh, cc, :], src, [("fbuf", gh // 8)], [("H", gh)])
        Kb_sb = [A.bf(KW) for _ in range(2)]
        Qb_sb = [A.bf(OWN_T) for _ in range(2)]
        maxch = max(r * (nb + 1) for r, nb in zip(RATES, nblk))
        Vb_sb = [A.bf(maxch * 128).rearrange("p (c d) -> p c d", d=128) for _ in range(2)]
        accb = A.f32(4 * OWN_T).rearrange("p (a t) -> p a t", t=OWN_T)
        PTb = [A.bf(512) for _ in range(3)]
        obf = [A.bf(OWN_T) for _ in range(2)]
        ptn = [0]
        for hp in range(4):
            for g in range(3):
                r = RATES[g]
                nb = nblk[g]
                for hh in range(2):
                    gh = g * 8 + hp * 2 + hh
                    dma("sp", Kb_sb[hh], KbT[gh, :, 0:KW], [], [("Kb", hh)])
                    dma("sp", Qb_sb[hh], QbT[gh, :, 0:OWN_T], [], [("Qb", hh)])
                    for rho in range(r):
                        base = HALO - 64 * r + rho
                        nrow = 128 * (nb + 1)
                        src = Vb[base:base + r * (nrow - 1) + 1:r, gh * 128:(gh + 1) * 128].rearrange("(j k) d -> k j d", k=128)
                        dma("sp", Vb_sb[hh][:, rho * (nb + 1):(rho + 1) * (nb + 1), :], src, [], [("Vb", hh)])
                for rho in range(r):
                    for j in range(nb):
                        bS = nextbank()
                        bO = nextbank()
                        firstS = True
                        for hh in range(2):
                            gh = g * 8 + hp * 2 + hh
                            for cc in range(2):
                                sub = ps[bS][:, (hh * 2 + cc) * 128:(hh * 2 + cc + 1) * 128]
                                kb0 = HALO + rho + r * (128 * (j + cc) - 64)
                                q0 = rho + r * 128 * j
                                mm(sub, Kb_sb[hh][:, kb0:kb0 + 127 * r + 1:r], Qb_sb[hh][:, q0:q0 + 127 * r + 1:r], firstS, False,
                                   [("Kb", hh), ("Qb", hh)], [("ps", bS)], gkey=("S", hh, cc), first=True, skip=True)
                                firstS = False
                                edge_l = (j == 0 and cc == 0)
                                edge_r = (j == nb - 1 and cc == 1)
                                mm(sub, Hsb[:, gh, cc, :], antiI, False, not (edge_l or edge_r),
                                   [("H", gh), ("c", "cmat")], [("ps", bS)], gkey=("S", hh, cc), first=False, skip=True)
                                if edge_l:
                                    mm(sub, mrows[0:1, 0:128], ones_row[0:1, :], False, not edge_r,
                                       [("c", "mrows"), ("c", "onesrow")], [("ps", bS)], gkey=("S", hh, cc), first=False, skip=True)
                                if edge_r:
                                    mm(sub, mrows[0:1, 128:256], ones_row[0:1, :], False, True,
                                       [("c", "mrows"), ("c", "onesrow")], [("ps", bS)], gkey=("S", hh, cc), first=False, skip=True)
                        pi = ptn[0] % 3
                        ptn[0] += 1
                        act(PTb[pi], ps[bS][:, :], AF.Exp, [("ps", bS)], [("PTb", pi)], scale=SCALE)
                        firstO = True
                        for cc in range(2):
                            ch = rho * (nb + 1) + j + cc
                            for hh in range(2):
                                pt = PTb[pi][:, (hh * 2 + cc) * 128:(hh * 2 + cc + 1) * 128]
                                mm(ps[bO][:, hh * 128:(hh + 1) * 128], Vb_sb[hh][:, ch, :], pt, firstO, cc == 1,
                                   [("Vb", hh), ("PTb", pi)], [("ps", bO)], gkey=("O", hh), first=(cc == 0), skip=True)
                                firstO = False
                                mm(ps[bO][:, 256 + hh * 128:256 + (hh + 1) * 128], ones_bf[:, :], pt, False, cc == 1,
                                   [("c", "ones"), ("PTb", pi)], [("ps", bO)], gkey=("L", hh), first=(cc == 0), skip=True)
                        q0 = rho + r * 128 * j
                        dst = accb[:, :, q0:q0 + 127 * r + 1:r]
                        srcp = ps[bO][:, :].rearrange("p (a q) -> p a q", q=128)
                        if g == 0:
                            cp("dve", dst, srcp, [("ps", bO)], [("accb",)])
                        else:
                            tt("dve", dst, srcp, dst, ALU.add, [("ps", bO), ("accb",)], [("accb",)])
            for hh in range(2):
                recip("dve", accb[:, 2 + hh, :], accb[:, 2 + hh, :], [("accb",)], [("accb",)])
                tt("dve", obf[hh], accb[:, hh, :], accb[:, 2 + hh, :], ALU.mult, [("accb",)], [("obf", hh)])
                h = hp * 2 + hh
                dma("pool", mixT[1024 + h * 128:1024 + (h + 1) * 128, 0:OWN_T], obf[hh], [("obf", hh)], [])
        P.barrier()

    if "C" in PH:
        B_ = phase_ffn_buffers()
        xs, hs, hid = B_["xs"], B_["hs"], B_["hid"]
        KmT = A.bf(4 * 256).rearrange("p (h m) -> p h m", m=256)
        Vm = A.bf(2 * 512).rearrange("p (c n) -> p c n", n=512)
        qm = A.bf(4 * TT).rearrange("p (h t) -> p h t", t=TT)
        om = A.bf(4 * TT).rearrange("p (h t) -> p h t", t=TT)
        PTm = [A.bf(TT) for _ in range(2)]
        rlm = A.f32(TT)
        cpn = [0]

        def evac2(out, in_, reads, writes):
            cpn[0] += 1
            if cpn[0] % 2:
                act(out, in_, AF.Copy, reads, writes)
            else:
                cp("dve", out, in_, reads, writes)

        dma("sp", xs[:, :, 0:256], memT.rearrange("(k p) t -> p k t", p=128), [], [("xs", kc) for kc in range(KC)])
        rmsnorm(B_, 3, ntok=256)
        s, wv = load_w(B_, "wkvm", 0, D, 0, 512)
        for h in range(4):
            b = nextbank()
            for kc in range(KC):
                mm(ps[b][:, 0:256], wv[:, kc, h * 128:(h + 1) * 128], hs[:, kc, 0:256], kc == 0, kc == KC - 1,
                   [("ws", s), ("hs", kc)], [("ps", b)])
            evac2(KmT[:, h, :], ps[b][:, 0:256], [("ps", b)], [("KmT",)])
        s, wv = load_w(B_, "wkvm", 0, D, 512, 1024)
        for mc in range(2):
            b = nextbank()
            for kc in range(KC):
                mm(ps[b][:, :], hs[:, kc, mc * 128:(mc + 1) * 128], wv[:, kc, :], kc == 0, kc == KC - 1,
                   [("ws", s), ("hs", kc)], [("ps", b)])
            evac2(Vm[:, mc, :], ps[b][:, :], [("ps", b)], [("Vm",)])

        for ti in range(NT_OWN):
            p0 = ti * TT
            dma("sp", xs, x1T[:, p0:p0 + TT].rearrange("(k p) t -> p k t", p=128), [], [("xs", kc) for kc in range(KC)])
            dma("sp", hs, mixT[:, p0:p0 + TT].rearrange("(k p) t -> p k t", p=128), [], [("hs", kc) for kc in range(KC)])
            for N in range(4):
                s, wv = load_w(B_, "wout", 0, D, N * 512, (N + 1) * 512)
                for n4 in range(4):
                    c = N * 4 + n4
                    b = nextbank()
                    for kc in range(KC):
                        mm(ps[b][:, :], wv[:, kc, n4 * 128:(n4 + 1) * 128], hs[:, kc, :], kc == 0, kc == KC - 1,
                           [("ws", s), ("hs", kc)], [("ps", b)])
                    tt("dve", xs[:, c, :], ps[b][:, :], xs[:, c, :], ALU.add, [("ps", b), ("xs", c)], [("xs", c)])
            rmsnorm(B_, 2)
            s, wv = load_w(B_, "wqm", 0, D, 0, 512)
            for h in range(4):
                b = nextbank()
                for kc in range(KC):
                    mm(ps[b][:, :], wv[:, kc, h * 128:(h + 1) * 128], hs[:, kc, :], kc == 0, kc == KC - 1,
                       [("ws", s), ("hs", kc)], [("ps", b)])
                evac2(qm[:, h, :], ps[b][:, :], [("ps", b)], [("qm", h)])
            for h in range(4):
                bo = nextbank()
                bl = nextbank()
                for mc in range(2):
                    b = nextbank()
                    mm(ps[b][:, :], KmT[:, h, mc * 128:(mc + 1) * 128], qm[:, h, :], True, True,
                       [("KmT",), ("qm", h)], [("ps", b)])
                    act(PTm[mc], ps[b][:, :], AF.Exp, [("ps", b)], [("PTm", mc)], scale=SCALE)
                    mm(ps[bo][:, :], Vm[:, mc, h * 128:(h + 1) * 128], PTm[mc], mc == 0, mc == 1,
                       [("Vm",), ("PTm", mc)], [("ps", bo)])
                    mm(ps[bl][:, :], ones_bf[:, :], PTm[mc], mc == 0, mc == 1,
                       [("c", "ones"), ("PTm", mc)], [("ps", bl)])
                recip("dve", rlm, ps[bl][:, :], [("ps", bl)], [("rlm",)])
                tt("dve", om[:, h, :], ps[bo][:, :], rlm, ALU.mult, [("ps", bo), ("rlm",)], [("om", h)])
            s, wv = load_w(B_, "wom", 0, 512, 0, D)
            for c in range(KC):
                b = nextbank()
                for h in range(4):
                    mm(ps[b][:, :], wv[:, h, c * 128:(c + 1) * 128], om[:, h, :], h == 0, h == 3,
                       [("ws", s), ("om", h)], [("ps", b)])
                tt("dve", xs[:, c, :], ps[b][:, :], xs[:, c, :], ALU.add, [("ps", b), ("xs", c)], [("xs", c)])
            rmsnorm(B_, 4)
            ffn(B_, "g2", "u2", "d2")
            rmsnorm(B_, 5, fp32_out=True)
            dma("pool", outT[:, p0:p0 + TT].rearrange("(k p) t -> p k t", p=128), xs, [("xs", kc) for kc in range(KC)], [])
    P.barrier()
    P.emit(nc, st)
    st.close()
    return nc, P


def rope_tables():
    nf = 32
    pos = np.arange(SEQ)
    row = (pos // 64).astype(np.float32)
    col = (pos % 64).astype(np.float32)
    inv = (np.float32(10000.0) ** (-np.arange(nf, dtype=np.float32) / np.float32(nf))).astype(np.float32)
    d = np.arange(128)
    axis = d // 64
    pair = (d % 64) // 32
    f = d % 32
    ang = np.where(axis[:, None] == 0, row[None, :], col[None, :]).astype(np.float32) * inv[f][:, None]
    cos = np.cos(ang).astype(np.float32)
    sin = np.sin(ang).astype(np.float32)
    sin_signed = np.where(pair[:, None] == 0, -sin, sin).astype(np.float32)
    return cos, sin_signed


def t5_buckets(rel):
    nb = 16
    max_exact = 8
    ret = (rel > 0).astype(np.int32) * nb
    n = np.abs(rel)
    large = max_exact + (np.log(np.maximum(n, 1) / max_exact) / np.log(1024 / max_exact) * (nb - max_exact)).astype(np.int32)
    large = np.minimum(large, nb - 1)
    return (ret + np.where(n < max_exact, n, large)).astype(np.int32)


def const_tables():
    d = np.arange(128)
    pair = (d % 64) // 32
    partner = np.where(pair == 0, d + 32, d - 32)
    ident = np.eye(128, dtype=np.float32)
    ropeT = np.zeros((128, 128), np.float32)
    ropeT[partner, d] = 1.0
    anti = np.zeros((128, 128), np.float32)
    anti[127 - d, d] = 1.0
    cmat = np.concatenate([ident, ropeT, anti], axis=1)
    oh = np.zeros((33, 3, 384), np.float32)
    for g in range(3):
        for u in range(384):
            m = u - 191
            if abs(m) <= 64:
                bkt = int(t5_buckets(np.array([RATES[g] * m]))[0])
                oh[bkt, g, u] = 1.0
            else:
                oh[32, g, u] = NEGM * SCALE
    return cmat, oh.reshape(33, 3 * 384)


def make_in_maps(inputs):
    f = lambda a: np.ascontiguousarray(np.asarray(a, dtype=np.float32))
    x = np.asarray(inputs["x"], dtype=np.float32)
    mem = np.asarray(inputs["mem"], dtype=np.float32)
    cos, sin_s = rope_tables()
    cmat, oh = const_tables()
    gl = [inputs[k] for k in ("ffn1_norm", "mix_norm", "mem_x_norm", "mem_m_norm", "ffn2_norm")]
    gl = [np.asarray(g, np.float32).reshape(-1) for g in gl] + [np.asarray(inputs["final_norm"], np.float32).reshape(-1)]
    gains = np.concatenate([g.reshape(KC, 128).T for g in gl], axis=1)
    qkg = np.stack([np.asarray(inputs["q_norm_a"], np.float32).reshape(-1),
                    np.asarray(inputs["k_norm_a"], np.float32).reshape(-1)], axis=1)
    shared = {
        "ffn1_w_gate": f(inputs["ffn1_w_gate"][0]), "ffn1_w_up": f(inputs["ffn1_w_up"][0]),
        "ffn1_w_down": f(inputs["ffn1_w_down"][0]), "w_in": f(inputs["w_in"][0]), "w_out": f(inputs["w_out"][0]),
        "w_q_mem": f(inputs["w_q_mem"][0]), "w_kv_mem": f(inputs["w_kv_mem"][0]), "w_o_mem": f(inputs["w_o_mem"][0]),
        "ffn2_w_gate": f(inputs["ffn2_w_gate"][0]), "ffn2_w_up": f(inputs["ffn2_w_up"][0]),
        "ffn2_w_down": f(inputs["ffn2_w_down"][0]),
        "gains": f(gains), "qkg": f(qkg), "cmat": f(cmat), "rel_bias": f(inputs["rel_bias"]), "ohaug": f(oh),
    }
    maps = []
    for c in range(8):
        b, r = c // 4, c % 4
        sh = OWN * r
        m = dict(shared)
        m["xT"] = np.ascontiguousarray(np.roll(x[b], -sh, axis=0).T)
        m["memT"] = np.ascontiguousarray(mem[b].T)
        m["cosT"] = np.ascontiguousarray(np.roll(cos, -sh, axis=1))
        m["sinT"] = np.ascontiguousarray(np.roll(sin_s, -sh, axis=1))
        mr = np.zeros((1, 256), np.float32)
        if r == 0:
            mr[0, 0:64] = NEGM
        if r == 3:
            mr[0, 128 + 64:256] = NEGM
        m["mrows"] = mr
        maps.append(m)
    return maps


_NC_CACHE = {}


def kernel(**inputs):
    if "nc" not in _NC_CACHE:
        _NC_CACHE["nc"] = build()[0]
    nc = _NC_CACHE["nc"]
    maps = make_in_maps(inputs)
    res = run_bass_kernel_spmd(nc, maps, core_ids=list(range(8)))
    out = np.empty((2, SEQ, D), np.float32)
    for c in range(8):
        b, r = c // 4, c % 4
        out[b, OWN * r:OWN * (r + 1), :] = np.asarray(res.results[c]["outT"]).T
    return out
```

```python
import numpy as np
from contextlib import ExitStack
import concourse.bass as bass
import concourse.mybir as mybir
from concourse.bass_utils import run_bass_kernel_spmd

F32 = mybir.dt.float32
BF16 = mybir.dt.bfloat16
ALU = mybir.AluOpType
AF = mybir.ActivationFunctionType

D = 2048
KC = 16
DFF = 5632
FC = 44
TT = 512
SEQ = 16384
OWN = 4096
HALO = 1024
IN_W = 10752
EPS = 1e-6
SCALE = 128 ** -0.5
NEGM = -3000.0
RATES = (1, 4, 16)
NBLK = (32, 8, 2)
CH_OFF = (0, 33, 33 + 36)
NCH = (33, 36, 48)


class Prog:
    CE = ("pe", "act", "dve", "pool")
    QE = ("sp", "act", "pool")
    NPOOL = 24

    def __init__(self):
        self.ops = []
        self.lw = {}
        self.rd_c = {}
        self.rd_d = {}
        self.last_c = {}
        self.dma_all = []
        self.log = None

    def add(self, eng, fn, reads=(), writes=(), dma=False):
        i = len(self.ops)
        deps = {}
        for k in reads:
            w = self.lw.get(k)
            if w is not None:
                deps[w] = "raw"
            if k[0] == "ps":
                for e2, r in self.rd_c.get(k, {}).items():
                    if e2 != eng:
                        deps.setdefault(r, "war")
        for k in writes:
            w = self.lw.get(k)
            if w is not None:
                deps.setdefault(w, "waw")
            for r in self.rd_c.get(k, {}).values():
                deps.setdefault(r, "war")
            for r in self.rd_d.get(k, ()):
                deps.setdefault(r, "war")
        deps.pop(i, None)
        self.ops.append(dict(eng=eng, fn=fn, dma=dma, deps=deps))
        for k in reads:
            if dma:
                self.rd_d.setdefault(k, []).append(i)
            else:
                self.rd_c.setdefault(k, {})[eng] = i
        for k in writes:
            self.lw[k] = i
            self.rd_c[k] = {}
            self.rd_d[k] = []
        if dma:
            self.dma_all.append(i)
        else:
            self.last_c[eng] = i
        return i

    def barrier(self):
        prev_c = dict(self.last_c)
        prev_d = list(self.dma_all)
        for e in ("pe", "act", "dve", "pool", "sp"):
            i = len(self.ops)
            deps = {j: "raw" for j in prev_c.values()}
            for j in prev_d:
                deps[j] = "raw"
            self.ops.append(dict(eng=e, fn=None, dma=False, deps=deps))
        self.lw = {}
        self.rd_c = {}
        self.rd_d = {}

    def emit(self, nc, st):
        ops = self.ops
        qcount = {q: 0 for q in self.QE}
        for o in ops:
            if o["dma"]:
                q = o["eng"]
                k = qcount[q]
                qcount[q] += 1
                o["dk"] = k
        esem = {e: st.enter_context(nc.semaphore("se_" + e)) for e in self.CE}
        qsem = {q: [st.enter_context(nc.semaphore("sq_%s_%d" % (q, j))) for j in range(self.NPOOL)]
                for q in self.QE}
        P = self.NPOOL
        need = [False] * len(ops)
        for i, o in enumerate(ops):
            kept = {}
            best_same = {}
            for j, kind in o["deps"].items():
                oj = ops[j]
                if oj["dma"]:
                    kept[j] = kind
                    continue
                if oj["fn"] is None and oj["eng"] == "sp":
                    continue
                if (not o["dma"]) and oj["eng"] == o["eng"]:
                    if o["eng"] == "pe" or o["fn"] is None:
                        continue
                e = oj["eng"]
                if e not in best_same or best_same[e] < j:
                    best_same[e] = j
            for e, j in best_same.items():
                kept[j] = "x"
            o["kept"] = kept
            for j in kept:
                if not ops[j]["dma"]:
                    need[j] = True
        ms = {}
        cnt = {e: 0 for e in self.CE}
        for i, o in enumerate(ops):
            if not o["dma"] and need[i] and o["eng"] in cnt:
                cnt[o["eng"]] += 1
                ms[i] = cnt[o["eng"]]
        self.stats = dict(cnt)

        def semval(j):
            oj = ops[j]
            if oj["dma"]:
                k = oj["dk"]
                return qsem[oj["eng"]][k % P], 16 * (k // P + 1)
            return esem[oj["eng"]], ms[j]

        def run(engname, eng):
            waited = {}
            for i, o in enumerate(ops):
                if o["eng"] != engname:
                    continue
                ws = []
                for j in o["kept"]:
                    ws.append(semval(j))
                if o["dma"]:
                    k = o["dk"]
                    if k >= P:
                        ws.append((qsem[engname][k % P], 16 * (k // P)))
                for s, v in ws:
                    key = id(s)
                    if waited.get(key, 0) >= v:
                        continue
                    waited[key] = v
                    eng.wait_ge(s, v)
                    if self.log is not None:
                        self.log.append((engname, i, "wait", s.name, v))
                if self.log is not None:
                    self.log.append((engname, i, "op", o.get("dk"), ms.get(i)))
                if o["fn"] is None:
                    if i in ms:
                        eng.nop().then_inc(esem[engname], 1)
                    continue
                ins = o["fn"](eng)
                if o["dma"]:
                    k = o["dk"]
                    ins.then_inc(qsem[engname][k % P], 16)
                elif i in ms:
                    ins.then_inc(esem[engname], 1)

        with nc.Block() as block:
            @block.tensor
            def _(e):
                run("pe", e)

            @block.scalar
            def _(e):
                run("act", e)

            @block.vector
            def _(e):
                run("dve", e)

            @block.gpsimd
            def _(e):
                run("pool", e)

            @block.sync
            def _(e):
                run("sp", e)


class Arena:
    def __init__(self, t, nwords):
        self.t = t
        self.n = nwords
        self.off = 0

    def f32(self, n):
        assert self.off + n <= self.n, ("arena overflow", self.off, n, self.n)
        ap = self.t[:, self.off:self.off + n]
        self.off += n
        return ap

    def bf(self, n):
        w = (n + 1) // 2
        assert self.off + w <= self.n, ("arena overflow", self.off, w, self.n)
        ap = self.t[:, self.off:self.off + w].bitcast(BF16)
        self.off += w
        return ap


ARENA_WORDS = 46 * 1024


def build(cfg=None):
    cfg = cfg or {}
    NT_ALL = cfg.get("nt_all", SEQ // TT)
    NT_OWN = cfg.get("nt_own", OWN // TT)
    DBG = cfg.get("dbg", False)
    PH = cfg.get("phases", "0ABDC")
    nc = bass.Bass("TRN2", target_bir_lowering=False)
    st = ExitStack()
    P = Prog()

    def din(name, shape, dt=F32):
        return nc.dram_tensor(name, list(shape), dt, kind="ExternalInput").ap()

    DUMP = cfg.get("dump", ())

    def dscr(name, shape, dt):
        kind = "ExternalOutput" if name in DUMP else ("ExternalInput" if name in cfg.get("ext_in", ()) else "Internal")
        return nc.dram_tensor(name, list(shape), dt, kind=kind).ap()

    xT = din("xT", [D, SEQ])
    memT = din("memT", [D, 256])
    w_in32 = {
        "g1": din("ffn1_w_gate", [D, DFF]), "u1": din("ffn1_w_up", [D, DFF]), "d1": din("ffn1_w_down", [DFF, D]),
        "win": din("w_in", [D, IN_W]), "wout": din("w_out", [D, D]),
        "wqm": din("w_q_mem", [D, 512]), "wkvm": din("w_kv_mem", [D, 1024]), "wom": din("w_o_mem", [512, D]),
        "g2": din("ffn2_w_gate", [D, DFF]), "u2": din("ffn2_w_up", [D, DFF]), "d2": din("ffn2_w_down", [DFF, D]),
    }
    gains_in = din("gains", [128, 6 * KC])
    qkg_in = din("qkg", [128, 2])
    cosT = din("cosT", [128, SEQ])
    sinT = din("sinT", [128, SEQ])
    cmat_in = din("cmat", [128, 3 * 128])
    mrows_in = din("mrows", [1, 256])
    relb_in = din("rel_bias", [32, 24])
    oh_in = din("ohaug", [33, 3 * 384])
    outT = nc.dram_tensor("outT", [D, OWN], F32, kind="ExternalOutput").ap()

    wbf = {k: dscr("bf_" + k, v.shape, BF16) for k, v in w_in32.items()}
    x1T = dscr("x1T", [D, OWN], F32)
    KaT = dscr("KaT", [2, 128, SEQ], BF16)
    Va = dscr("Va", [SEQ, 256], BF16)
    QaT = dscr("QaT", [8, 128, OWN], BF16)
    QbT = dscr("QbT", [24, 128, OWN], BF16)
    KbT = dscr("KbT", [24, 128, OWN + 2 * HALO], BF16)
    Vb = dscr("Vb", [OWN + 2 * HALO, 3072], BF16)
    mixT = dscr("mixT", [D, OWN], BF16)
    fbuf = dscr("fbuf", [24, 384], BF16)

    arena_t = st.enter_context(nc.sbuf_tensor("arena", [128, ARENA_WORDS], F32))
    ones_bf = st.enter_context(nc.sbuf_tensor("ones_bf", [128, 128], BF16))
    cmat = st.enter_context(nc.sbuf_tensor("cmat_sb", [128, 3 * 128], BF16))
    gains = st.enter_context(nc.sbuf_tensor("gains_sb", [128, 6 * KC], F32))
    qkg = st.enter_context(nc.sbuf_tensor("qkg_sb", [128, 2], F32))
    mrows = st.enter_context(nc.sbuf_tensor("mrows_sb", [1, 256], BF16))
    ones_row = st.enter_context(nc.sbuf_tensor("ones_row", [1, 128], BF16))
    wones = st.enter_context(nc.sbuf_tensor("wones", [128, 512], BF16))
    relb = st.enter_context(nc.sbuf_tensor("relb_sb", [33, 24], F32))
    ohaug = st.enter_context(nc.sbuf_tensor("ohaug_sb", [33, 3 * 384], F32))
    ps = [st.enter_context(nc.psum_tensor("ps%d" % i, [128, 512], F32)) for i in range(8)]
    ident = cmat[:, 0:128]
    ropeT = cmat[:, 128:256]
    antiI = cmat[:, 256:384]
    A = Arena(arena_t, ARENA_WORDS)

    psn = [0]

    def nextbank():
        b = psn[0] % 8
        psn[0] += 1
        return b

    STQ = cfg.get("stq", "pool")

    def dma(q, out, in_, reads, writes):
        if q == "pool" and reads:
            q = STQ
        return P.add(q, lambda e, o=out, i=in_: e.dma_start(out=o, in_=i), reads=reads, writes=writes, dma=True)

    open_grp = {}

    def mm(out, lhsT, rhs, start, stop, reads, writes, gkey=None, first=None, skip=False):
        if first is None:
            first = start
        i = P.add("pe", lambda e, o=out, l=lhsT, r=rhs, s0=start, s1=stop, sk=skip: e.matmul(
            o, l, r, start=s0, stop=s1, skip_group_check=sk), reads=reads, writes=writes)
        gk = gkey if gkey is not None else tuple(writes)
        start = first
        if start:
            open_grp[gk] = []
        open_grp[gk].append(i)
        if stop:
            for j in open_grp.pop(gk):
                P.ops[j]["redir"] = i
        return i

    def act(out, in_, func, reads, writes, scale=1.0, bias=0.0):
        return P.add("act", lambda e, o=out, i=in_, f=func, s=scale, b=bias: e.activation(o, i, f, bias=b, scale=s),
                     reads=reads, writes=writes)

    def ts(eng, out, in0, s1, s2, op0, op1, reads, writes):
        if op1 is None:
            return P.add(eng, lambda e, o=out, i=in0, a=s1, p0=op0: e.tensor_scalar(o, i, a, None, p0),
                         reads=reads, writes=writes)
        return P.add(eng, lambda e, o=out, i=in0, a=s1, b=s2, p0=op0, p1=op1: e.tensor_scalar(o, i, a, b, p0, p1),
                     reads=reads, writes=writes)

    def stt(eng, out, in0, sc, in1, op0, op1, reads, writes):
        return P.add(eng, lambda e, o=out, i=in0, s=sc, j=in1, p0=op0, p1=op1: e.scalar_tensor_tensor(o, i, s, j, p0, p1),
                     reads=reads, writes=writes)

    def tt(eng, out, in0, in1, op, reads, writes):
        return P.add(eng, lambda e, o=out, i=in0, j=in1, p=op: e.tensor_tensor(o, i, j, p), reads=reads, writes=writes)

    def cp(eng, out, in_, reads, writes):
        return P.add(eng, lambda e, o=out, i=in_: e.tensor_copy(o, i), reads=reads, writes=writes)

    def transp(out, in_, reads, writes):
        return P.add("pe", lambda e, o=out, i=in_: e.transpose(o, i, ident), reads=reads + [("c", "cmat")], writes=writes)

    def recip(eng, out, in_, reads, writes):
        return P.add(eng, lambda e, o=out, i=in_: e.reciprocal(o, i), reads=reads, writes=writes)

    def memset(eng, ap, val, writes):
        return P.add(eng, lambda e, a=ap, v=val: e.memset(a, v), reads=(), writes=writes)

    memset("pool", ones_bf[:, :], 1.0, [("c", "ones")])
    memset("pool", ones_row[:, :], 1.0, [("c", "onesrow")])
    memset("pool", wones[:, :], 1.0, [("c", "wones")])
    dma("pool", cmat[:, :], cmat_in, [], [("c", "cmat")])
    dma("pool", mrows[:, :], mrows_in, [], [("c", "mrows")])
    dma("sp", gains[:, :], gains_in, [], [("c", "gains")])
    dma("sp", qkg[:, :], qkg_in, [], [("c", "qkg")])
    dma("sp", relb[0:32, :], relb_in, [], [("c", "relb")])
    memset("pool", relb[32:33, :], 1.0, [("c", "relb1")])
    dma("sp", ohaug[:, :], oh_in, [], [("c", "ohaug")])

    def wkeys(name, r0, r1):
        return [("w", name, rc) for rc in range(r0 // 128, (r1 + 127) // 128)]

    def convert(name):
        src, dst = w_in32[name], wbf[name]
        rows, cols = src.shape
        rb = 256 if cols > 4096 else 512
        rb = min(rb, rows)
        for r0 in range(0, rows, rb):
            dma("pool", dst[r0:r0 + rb, :], src[r0:r0 + rb, :], [], wkeys(name, r0, r0 + rb))

    if "0" in PH:
        for name in ("g1", "u1", "d1", "win", "wout", "wqm", "wkvm", "wom", "g2", "u2", "d2"):
            convert(name)

    def phase_ffn_buffers():
        A.off = 0
        B_ = {}
        B_["xs"] = A.f32(KC * TT).rearrange("p (k t) -> p k t", t=TT)
        B_["hid"] = A.bf(FC * TT).rearrange("p (k t) -> p k t", t=TT)
        B_["hs"] = A.bf(KC * TT).rearrange("p (k t) -> p k t", t=TT)
        B_["ws"] = [A.bf(8192) for _ in range(3 if cfg.get("sqsep") else 4)]
        if cfg.get("sqsep"):
            B_["sq"] = A.bf(KC * TT).rearrange("p (k t) -> p k t", t=TT)
        B_["rstd"] = A.f32(TT)
        B_["sg"] = [A.f32(TT) for _ in range(2)]
        return B_

    wsn = [0]

    def load_w(B_, name, r0, r1, c0, c1):
        s = wsn[0] % len(B_["ws"])
        wsn[0] += 1
        nk = (r1 - r0) // 128
        ncol = c1 - c0
        dst = B_["ws"][s][:, 0:nk * ncol].rearrange("p (k c) -> p k c", c=ncol)
        src = wbf[name][r0:r1, c0:c1].rearrange("(k p) c -> p k c", p=128)
        dma("sp", dst, src, wkeys(name, r0, r1), [("ws", s)])
        return s, dst

    WARM1 = cfg.get("warm1", 10)
    WARM2 = cfg.get("warm2", 36)

    def warm(n):
        if n <= 0:
            return
        b = nextbank()
        for _ in range(n):
            mm(ps[b][:, :], ones_bf[:, :], wones[:, :], True, True, [("c", "ones"), ("c", "wones")], [("ps", b)])

    def rmsnorm(B_, gidx, ntok=TT, fp32_out=None):
        xs, hid, hs, rstd = B_["xs"], B_["hid"], B_["hs"], B_["rstd"]
        NS = cfg.get("nstep", 99) if gidx == 1 else 99
        hkey = "hid"
        if "sq" in B_:
            hid = B_["sq"]
            hkey = "sq"
        if NS < 1:
            return
        for q in range(4):
            rk = [("xs", kc) for kc in range(q * 4, q * 4 + 4)]
            wk = [(hkey, kc) for kc in range(q * 4, q * 4 + 4)]
            if q % 2 == 0:
                act(hid[:, q * 4:(q + 1) * 4, 0:ntok], xs[:, q * 4:(q + 1) * 4, 0:ntok], AF.Square, rk, wk)
            else:
                tt("dve", hid[:, q * 4:(q + 1) * 4, 0:ntok], xs[:, q * 4:(q + 1) * 4, 0:ntok], xs[:, q * 4:(q + 1) * 4, 0:ntok],
                   ALU.mult, rk, wk)
        warm(WARM1)
        b = nextbank()
        for kc in range(KC):
            mm(ps[b][:, 0:ntok], ones_bf[:, :], hid[:, kc, 0:ntok], kc == 0, kc == KC - 1,
               [(hkey, kc), ("c", "ones")], [("ps", b)])
        warm(WARM2)
        act(rstd[:, 0:ntok], ps[b][:, 0:ntok], AF.Sqrt, [("ps", b)], [("rstd",)], scale=1.0 / D, bias=EPS)
        recip("dve", rstd[:, 0:ntok], rstd[:, 0:ntok], [("rstd",)], [("rstd",)])
        for kc in range(KC):
            eng = "dve"
            if fp32_out is None:
                stt(eng, hs[:, kc, 0:ntok], xs[:, kc, 0:ntok], gains[:, gidx * KC + kc:gidx * KC + kc + 1], rstd[:, 0:ntok],
                    ALU.mult, ALU.mult, [("xs", kc), ("rstd",), ("c", "gains")], [("hs", kc)])
            else:
                stt(eng, xs[:, kc, 0:ntok], xs[:, kc, 0:ntok], gains[:, gidx * KC + kc:gidx * KC + kc + 1], rstd[:, 0:ntok],
                    ALU.mult, ALU.mult, [("xs", kc), ("rstd",), ("c", "gains")], [("xs", kc)])

    def ffn(B_, wg, wu, wd):
        xs, hid, hs, sgt = B_["xs"], B_["hid"], B_["hs"], B_["sg"]
        for J in range(FC // 4):
            s_g, wgv = load_w(B_, wg, 0, D, J * 512, (J + 1) * 512)
            s_u, wuv = load_w(B_, wu, 0, D, J * 512, (J + 1) * 512)
            for jj in range(4):
                j = J * 4 + jj
                bg = nextbank()
                bu = nextbank()
                for kc in range(KC):
                    mm(ps[bg][:, :], wgv[:, kc, jj * 128:(jj + 1) * 128], hs[:, kc, :], kc == 0, kc == KC - 1,
                       [("ws", s_g), ("hs", kc)], [("ps", bg)])
                for kc in range(KC):
                    mm(ps[bu][:, :], wuv[:, kc, jj * 128:(jj + 1) * 128], hs[:, kc, :], kc == 0, kc == KC - 1,
                       [("ws", s_u), ("hs", kc)], [("ps", bu)])
                si = j % 2
                act(sgt[si], ps[bg][:, :], AF.Silu, [("ps", bg)], [("sg", si)])
                tt("dve", hid[:, j, :], sgt[si], ps[bu][:, :], ALU.mult, [("sg", si), ("ps", bu)], [("hid", j)])
        for N in range(4):
            banks = [nextbank() for _ in range(4)]
            for q in range(4):
                s_d, wdv = load_w(B_, wd, q * 11 * 128, (q + 1) * 11 * 128, N * 512, (N + 1) * 512)
                for kl in range(11):
                    k = q * 11 + kl
                    for n4 in range(4):
                        mm(ps[banks[n4]][:, :], wdv[:, kl, n4 * 128:(n4 + 1) * 128], hid[:, k, :], k == 0, k == FC - 1,
                           [("ws", s_d), ("hid", k)], [("ps", banks[n4])])
            for n4 in range(4):
                c = N * 4 + n4
                stt("dve", xs[:, c, :], ps[banks[n4]][:, :], 0.5, xs[:, c, :], ALU.mult, ALU.add,
                    [("ps", banks[n4]), ("xs", c)], [("xs", c)])

    if "A" in PH:
        B_ = phase_ffn_buffers()
        xs, hs = B_["xs"], B_["hs"]
        ctab = A.f32(TT)
        stab = A.f32(TT)
        t1 = A.f32(TT)
        t2 = A.f32(TT)
        rs2 = A.f32(TT)
        kgb = A.bf(TT)
        sqk = A.bf(TT)
        kr = [A.bf(TT) for _ in range(2)]
        stg = [A.bf(TT) for _ in range(3)]
        vst = [A.bf(4 * 256).rearrange("p (a c) -> p a c", c=256) for _ in range(2)]
        cnt = {"kr": 0, "stg": 0, "vst": 0, "cp": 0}

        def normrope(b, gi, dst):
            NR = cfg.get("nr", 99)
            if NR < 1:
                return
            act(sqk, ps[b][:, :], AF.Square, [("ps", b)], [("sqk",)])
            if NR < 2:
                return
            ts("dve", kgb, ps[b][:, :], qkg[:, gi:gi + 1], None, ALU.mult, None, [("ps", b), ("c", "qkg")], [("kgb",)])
            if NR < 3:
                return
            b2 = nextbank()
            mm(ps[b2][:, :], ones_bf[:, :], sqk, True, True, [("sqk",), ("c", "ones")], [("ps", b2)])
            if NR < 4:
                return
            b3 = nextbank()
            mm(ps[b3][:, :], ropeT, kgb, True, True, [("kgb",), ("c", "cmat")], [("ps", b3)])
            if NR < 5:
                return
            VAR = cfg.get("var", 0)
            if VAR == 1:
                act(rs2, ps[b2][:, :], AF.Copy, [("ps", b2)], [("rs2",)])
            elif VAR == 2:
                act(rs2, ps[b2][:, :], AF.Sqrt, [("ps", b2)], [("rs2",)], scale=1.0 / 128, bias=EPS)
                return
            elif VAR == 3:
                act(rs2, ps[b2][:, :], AF.Sqrt, [("ps", b2)], [("rs2",)], scale=1.0 / D, bias=EPS)
            else:
                act(rs2, ps[b2][:, :], AF.Sqrt, [("ps", b2)], [("rs2",)], scale=1.0 / 128, bias=EPS)
            recip("dve", rs2, rs2, [("rs2",)], [("rs2",)])
            if NR < 6:
                return
            stt("dve", t1, ps[b][:, :], qkg[:, gi:gi + 1], ctab, ALU.mult, ALU.mult, [("ps", b), ("ctab",), ("c", "qkg")], [("t1",)])
            tt("dve", t2, ps[b3][:, :], stab, ALU.mult, [("ps", b3), ("stab",)], [("t2",)])
            tt("dve", t1, t1, t2, ALU.add, [("t1",), ("t2",)], [("t1",)])
            if NR < 7:
                return
            si = cnt["kr"] % 2
            cnt["kr"] += 1
            tt("dve", kr[si], t1, rs2, ALU.mult, [("t1",), ("rs2",)], [("kr", si)])
            if NR < 8:
                return
            dma("pool", dst, kr[si], [("kr", si)], [])

        def evac_copy(out, in_, reads, writes):
            cnt["cp"] += 1
            if cnt["cp"] % 2:
                act(out, in_, AF.Copy, reads, writes)
            else:
                cp("dve", out, in_, reads, writes)

        def proj_fm_heads(c0, nheads, dst_fn):
            for h0 in range(0, nheads, 4):
                s, wv = load_w(B_, "win", 0, D, c0 + h0 * 128, c0 + (h0 + 4) * 128)
                for hh in range(4):
                    b = nextbank()
                    for kc in range(KC):
                        mm(ps[b][:, :], wv[:, kc, hh * 128:(hh + 1) * 128], hs[:, kc, :], kc == 0, kc == KC - 1,
                           [("ws", s), ("hs", kc)], [("ps", b)])
                    dst_fn(h0 + hh, b)

        def plain_store(dst):
            def f(b):
                si = cnt["stg"] % 3
                cnt["stg"] += 1
                evac_copy(stg[si], ps[b][:, :], [("ps", b)], [("stg", si)])
                dma("pool", dst, stg[si], [("stg", si)], [])
            return f

        for ti in list(range(NT_ALL)):
            p0 = ti * TT
            own = ti < NT_OWN
            halo = (NT_OWN <= ti < NT_OWN + 2) or ti >= NT_ALL - 2
            pos0 = p0 if ti < NT_OWN + 2 else p0 - NT_ALL * TT
            dma("sp", xs, xT[:, p0:p0 + TT].rearrange("(k p) t -> p k t", p=128), [], [("xs", kc) for kc in range(KC)])
            STOP = cfg.get("stop", 99)
            if STOP < 1:
                continue
            rmsnorm(B_, 0)
            if STOP < 2:
                continue
            ffn(B_, "g1", "u1", "d1")
            if STOP < 3:
                continue
            if own:
                dma("pool", x1T[:, p0:p0 + TT].rearrange("(k p) t -> p k t", p=128), xs, [("xs", kc) for kc in range(KC)], [])
            if STOP < 3.5:
                continue
            rmsnorm(B_, 1)
            if STOP < 4:
                continue
            dma("sp", ctab, cosT[:, p0:p0 + TT], [], [("ctab",)])
            dma("sp", stab, sinT[:, p0:p0 + TT], [], [("stab",)])
            s, wv = load_w(B_, "win", 0, D, 1024, 1536)
            for hd in range(2):
                b = nextbank()
                for kc in range(KC):
                    mm(ps[b][:, :], wv[:, kc, hd * 128:(hd + 1) * 128], hs[:, kc, :], kc == 0, kc == KC - 1,
                       [("ws", s), ("hs", kc)], [("ps", b)])
                normrope(b, 1, KaT[hd, :, p0:p0 + TT])
            if STOP < 4.1:
                continue
            vi = cnt["vst"] % 2
            cnt["vst"] += 1
            for tb in range(4):
                b = nextbank()
                for kc in range(KC):
                    mm(ps[b][:, 0:256], hs[:, kc, tb * 128:(tb + 1) * 128], wv[:, kc, 256:512], kc == 0, kc == KC - 1,
                       [("ws", s), ("hs", kc)], [("ps", b)])
                evac_copy(vst[vi][:, tb, :], ps[b][:, 0:256], [("ps", b)], [("vst", vi)])
            dma("pool", Va[p0:p0 + TT, :].rearrange("(a p) c -> p a c", p=128), vst[vi], [("vst", vi)], [])
            if STOP < 4.2:
                continue
            if own:
                proj_fm_heads(0, 8, lambda h, b: normrope(b, 0, QaT[h, :, p0:p0 + TT]))
                if STOP < 4.3:
                    continue
                proj_fm_heads(1536, 24, lambda h, b: plain_store(QbT[h, :, p0:p0 + TT])(b))
            if STOP < 4.4:
                continue
            if own or halo:
                c0 = pos0 + HALO
                proj_fm_heads(4608, 24, lambda h, b: plain_store(KbT[h, :, c0:c0 + TT])(b))
                if STOP < 4.5:
                    continue
                for i6 in range(6):
                    s, wv = load_w(B_, "win", 0, D, 7680 + i6 * 512, 7680 + (i6 + 1) * 512)
                    for tb in range(4):
                        b = nextbank()
                        for kc in range(KC):
                            mm(ps[b][:, :], hs[:, kc, tb * 128:(tb + 1) * 128], wv[:, kc, :], kc == 0, kc == KC - 1,
                               [("ws", s), ("hs", kc)], [("ps", b)])
                        plain_store(Vb[c0 + tb * 128:c0 + (tb + 1) * 128, i6 * 512:(i6 + 1) * 512])(b)
        P.barrier()


    OWN_T = NT_OWN * TT
    if "B" in PH:
        A.off = 0
        NKC = NT_ALL * 4
        NQB = NT_OWN * 4
        SK = NT_ALL * TT
        Kt = [A.bf(SK) for _ in range(2)]
        Vt = [A.bf(NKC * 129).rearrange("p (c d) -> p c d", d=129) for _ in range(2)]
        Qt = [A.bf(512) for _ in range(2)]
        PT = [A.bf(512) for _ in range(4)]
        osb = [A.bf(512).rearrange("p (h d) -> p h d", d=128) for _ in range(2)]
        rl = [A.f32(4) for _ in range(2)]
        mst = [A.bf(512) for _ in range(2)]
        KPC = 8
        for g in range(2):
            memset("pool", Vt[g][:, :, 128:129], 1.0, [("V1", g)])
            for pc in range(NKC // KPC):
                c0 = pc * KPC
                dma("sp", Kt[g][:, c0 * 128:(c0 + KPC) * 128], KaT[g, :, c0 * 128:(c0 + KPC) * 128], [], [("K", g, pc)])
                dma("sp", Vt[g][:, c0:c0 + KPC, 0:128],
                    Va[c0 * 128:(c0 + KPC) * 128, g * 128:(g + 1) * 128].rearrange("(c p) d -> p c d", p=128),
                    [], [("V", g, pc)])
        its = [(g, qb, kc) for g in range(2) for qb in range(NQB) for kc in range(NKC)]
        sbank = [0]

        def issue_S(idx):
            g, qb, kc = its[idx]
            blk = g * NQB + qb
            qs = blk % 2
            if kc == 0:
                dma("sp", Qt[qs].rearrange("p (h q) -> p h q", q=128),
                    QaT[4 * g:4 * g + 4, :, qb * 128:(qb + 1) * 128].rearrange("h d q -> d h q"), [], [("Qt", qs)])
            b = sbank[0] % 4
            sbank[0] += 1
            mm(ps[b][:, :], Kt[g][:, kc * 128:(kc + 1) * 128], Qt[qs], True, True,
               [("K", g, kc // KPC), ("Qt", qs)], [("ps", b)])
            pi = idx % 4
            act(PT[pi], ps[b][:, :], AF.Exp, [("ps", b)], [("PT", pi)], scale=SCALE)

        def issue_PV(idx):
            g, qb, kc = its[idx]
            blk = g * NQB + qb
            par = blk % 2
            pi = idx % 4
            for hh in range(4):
                bank = 4 + 2 * par + hh // 2
                c0 = (hh % 2) * 256
                mm(ps[bank][:, c0:c0 + 129], PT[pi][:, hh * 128:(hh + 1) * 128], Vt[g][:, kc, :],
                   (kc == 0 and hh % 2 == 0), kc == NKC - 1,
                   [("PT", pi), ("V", g, kc // KPC), ("V1", g)], [("ps", bank)], gkey=("acc", par, hh), first=(kc == 0), skip=True)
            if kc == NKC - 1:
                for hh in range(4):
                    bank = 4 + 2 * par + hh // 2
                    c0 = (hh % 2) * 256
                    recip("dve", rl[par][:, hh:hh + 1], ps[bank][:, c0 + 128:c0 + 129], [("ps", bank)], [("rl", par, hh)])
                    ts("dve", osb[par][:, hh, :], ps[bank][:, c0:c0 + 128], rl[par][:, hh:hh + 1], None, ALU.mult, None,
                       [("ps", bank), ("rl", par, hh)], [("osb", par, hh)])
                b = sbank[0] % 4
                sbank[0] += 1
                pst = ps[b].bitcast(BF16)
                for hh in range(4):
                    transp(pst[:, hh * 128:(hh + 1) * 128], osb[par][:, hh, :], [("osb", par, hh)], [("ps", b)])
                cp("dve", mst[par], pst[:, 0:512], [("ps", b)], [("mst", par)])
                dma("pool", mixT[4 * g * 128:(4 * g + 4) * 128, qb * 128:(qb + 1) * 128].rearrange("(h d) q -> d h q", d=128),
                    mst[par].rearrange("p (h q) -> p h q", q=128), [("mst", par)], [])

        LA = 2
        for idx in range(min(LA, len(its))):
            issue_S(idx)
        for idx in range(len(its)):
            if idx + LA < len(its):
                issue_S(idx + LA)
            issue_PV(idx)
        P.barrier()

    if "D" in PH:
        A.off = 0
        KW = OWN_T + 2 * HALO
        nblk = [OWN_T // (128 * r) for r in RATES]
        fb_sb = A.bf(384)
        for g in range(3):
            b = nextbank()
            mm(ps[b][0:8, 0:384], relb[0:33, g * 8:(g + 1) * 8], ohaug[0:33, g * 384:(g + 1) * 384], True, True,
               [("c", "relb"), ("c", "relb1"), ("c", "ohaug")], [("ps", b)])
            act(fb_sb[0:8, :], ps[b][0:8, 0:384], AF.Copy, [("ps", b)], [("fb",)], scale=1.0 / SCALE)
            dma("pool", fbuf[g * 8:(g + 1) * 8, :], fb_sb[0:8, :], [("fb",)], [("fbuf", g)])
        Hsb = A.bf(24 * 2 * 128).rearrange("p (a c k) -> p a c k", c=2, k=128)
        fb_t = fbuf.tensor
        for gh in range(24):
            for cc in range(2):
                src = bass.AP(tensor=fb_t, offset=gh * 384 + 128 * cc, ap=[[1, 128], [1, 128]])
                dma("sp", Hsb[:, gh, cc, :], src, [("fbuf", gh // 8)], [("H", gh)])
        Kb_sb = [A.bf(KW) for _ in range(2)]
        Qb_sb = [A.bf(OWN_T) for _ in range(2)]
        maxch = max(r * (nb + 1) for r, nb in zip(RATES, nblk))
        Vb_sb2 = [[A.bf(maxch * 128).rearrange("p (c d) -> p c d", d=128) for _ in range(2)] for _ in range(2)]
        itn = [0]
        accb = A.f32(4 * OWN_T).rearrange("p (a t) -> p a t", t=OWN_T)
        PTb = [A.bf(512) for _ in range(3)]
        obf1 = A.bf(OWN_T)
        obf = [obf1, obf1]
        ptn = [0]
        for hp in range(4):
            for g in range(3):
                r = RATES[g]
                nb = nblk[g]
                vpar = itn[0] % 2
                itn[0] += 1
                Vb_sb = Vb_sb2[vpar]
                for hh in range(2):
                    gh = g * 8 + hp * 2 + hh
                    for rho in range(r):
                        base = HALO - 64 * r + rho
                        nrow = 128 * (nb + 1)
                        src = Vb[base:base + r * (nrow - 1) + 1:r, gh * 128:(gh + 1) * 128].rearrange("(j k) d -> k j d", k=128)
                        dma("sp", Vb_sb[hh][:, rho * (nb + 1):(rho + 1) * (nb + 1), :], src, [], [("Vb", vpar, hh)])
                for hh in range(2):
                    gh = g * 8 + hp * 2 + hh
                    dma("sp", Kb_sb[hh], KbT[gh, :, 0:KW], [], [("Kb", hh)])
                    dma("sp", Qb_sb[hh], QbT[gh, :, 0:OWN_T], [], [("Qb", hh)])
                for rho in range(r):
                    for j in range(nb):
                        bS = nextbank()
                        bO = nextbank()
                        firstS = True
                        for hh in range(2):
                            gh = g * 8 + hp * 2 + hh
                            for cc in range(2):
                                sub = ps[bS][:, (hh * 2 + cc) * 128:(hh * 2 + cc + 1) * 128]
                                kb0 = HALO + rho + r * (128 * (j + cc) - 64)
                                q0 = rho + r * 128 * j
                                mm(sub, Kb_sb[hh][:, kb0:kb0 + 127 * r + 1:r], Qb_sb[hh][:, q0:q0 + 127 * r + 1:r], firstS, False,
                                   [("Kb", hh), ("Qb", hh)], [("ps", bS)], gkey=("S", hh, cc), first=True, skip=True)
                                firstS = False
                                edge_l = (j == 0 and cc == 0)
                                edge_r = (j == nb - 1 and cc == 1)
                                mm(sub, Hsb[:, gh, cc, :], antiI, False, not (edge_l or edge_r),
                                   [("H", gh), ("c", "cmat")], [("ps", bS)], gkey=("S", hh, cc), first=False, skip=True)
                                if edge_l:
                                    mm(sub, mrows[0:1, 0:128], ones_row[0:1, :], False, not edge_r,
                                       [("c", "mrows"), ("c", "onesrow")], [("ps", bS)], gkey=("S", hh, cc), first=False, skip=True)
                                if edge_r:
                                    mm(sub, mrows[0:1, 128:256], ones_row[0:1, :], False, True,
                                       [("c", "mrows"), ("c", "onesrow")], [("ps", bS)], gkey=("S", hh, cc), first=False, skip=True)
                        pi = ptn[0] % 3
                        ptn[0] += 1
                        act(PTb[pi], ps[bS][:, :], AF.Exp, [("ps", bS)], [("PTb", pi)], scale=SCALE)
                        firstO = True
                        for cc in range(2):
                            ch = rho * (nb + 1) + j + cc
                            for hh in range(2):
                                pt = PTb[pi][:, (hh * 2 + cc) * 128:(hh * 2 + cc + 1) * 128]
                                mm(ps[bO][:, hh * 128:(hh + 1) * 128], Vb_sb[hh][:, ch, :], pt, firstO, cc == 1,
                                   [("Vb", vpar, hh), ("PTb", pi)], [("ps", bO)], gkey=("O", hh), first=(cc == 0), skip=True)
                                firstO = False
                                mm(ps[bO][:, 256 + hh * 128:256 + (hh + 1) * 128], ones_bf[:, :], pt, False, cc == 1,
                                   [("c", "ones"), ("PTb", pi)], [("ps", bO)], gkey=("L", hh), first=(cc == 0), skip=True)
                        q0 = rho + r * 128 * j
                        dst = accb[:, :, q0:q0 + 127 * r + 1:r]
                        srcp = ps[bO][:, :].rearrange("p (a q) -> p a q", q=128)
                        if g == 0:
                            cp("dve", dst, srcp, [("ps", bO)], [("accb",)])
                        else:
                            tt("dve", dst, srcp, dst, ALU.add, [("ps", bO), ("accb",)], [("accb",)])
            for hh in range(2):
                recip("dve", accb[:, 2 + hh, :], accb[:, 2 + hh, :], [("accb",)], [("accb",)])
                tt("dve", obf[hh], accb[:, hh, :], accb[:, 2 + hh, :], ALU.mult, [("accb",)], [("obf", 0)])
                h = hp * 2 + hh
                dma("pool", mixT[1024 + h * 128:1024 + (h + 1) * 128, 0:OWN_T], obf[hh], [("obf", 0)], [])
        P.barrier()

    if "C" in PH:
        B_ = phase_ffn_buffers()
        xs, hs, hid = B_["xs"], B_["hs"], B_["hid"]
        KmT = A.bf(4 * 256).rearrange("p (h m) -> p h m", m=256)
        Vm = A.bf(2 * 512).rearrange("p (c n) -> p c n", n=512)
        qm = A.bf(4 * TT).rearrange("p (h t) -> p h t", t=TT)
        om = A.bf(4 * TT).rearrange("p (h t) -> p h t", t=TT)
        PTm = [A.bf(TT) for _ in range(2)]
        rlm = A.f32(TT)
        cpn = [0]

        def evac2(out, in_, reads, writes):
            cpn[0] += 1
            if cpn[0] % 2:
                act(out, in_, AF.Copy, reads, writes)
            else:
                cp("dve", out, in_, reads, writes)

        dma("sp", xs[:, :, 0:256], memT.rearrange("(k p) t -> p k t", p=128), [], [("xs", kc) for kc in range(KC)])
        rmsnorm(B_, 3, ntok=256)
        s, wv = load_w(B_, "wkvm", 0, D, 0, 512)
        for h in range(4):
            b = nextbank()
            for kc in range(KC):
                mm(ps[b][:, 0:256], wv[:, kc, h * 128:(h + 1) * 128], hs[:, kc, 0:256], kc == 0, kc == KC - 1,
                   [("ws", s), ("hs", kc)], [("ps", b)])
            evac2(KmT[:, h, :], ps[b][:, 0:256], [("ps", b)], [("KmT",)])
        s, wv = load_w(B_, "wkvm", 0, D, 512, 1024)
        for mc in range(2):
            b = nextbank()
            for kc in range(KC):
                mm(ps[b][:, :], hs[:, kc, mc * 128:(mc + 1) * 128], wv[:, kc, :], kc == 0, kc == KC - 1,
                   [("ws", s), ("hs", kc)], [("ps", b)])
            evac2(Vm[:, mc, :], ps[b][:, :], [("ps", b)], [("Vm",)])

        for ti in range(NT_OWN):
            p0 = ti * TT
            dma("sp", xs, x1T[:, p0:p0 + TT].rearrange("(k p) t -> p k t", p=128), [], [("xs", kc) for kc in range(KC)])
            dma("sp", hs, mixT[:, p0:p0 + TT].rearrange("(k p) t -> p k t", p=128), [], [("hs", kc) for kc in range(KC)])
            for N in range(4):
                s, wv = load_w(B_, "wout", 0, D, N * 512, (N + 1) * 512)
                for n4 in range(4):
                    c = N * 4 + n4
                    b = nextbank()
                    for kc in range(KC):
                        mm(ps[b][:, :], wv[:, kc, n4 * 128:(n4 + 1) * 128], hs[:, kc, :], kc == 0, kc == KC - 1,
                           [("ws", s), ("hs", kc)], [("ps", b)])
                    tt("dve", xs[:, c, :], ps[b][:, :], xs[:, c, :], ALU.add, [("ps", b), ("xs", c)], [("xs", c)])
            rmsnorm(B_, 2)
            s, wv = load_w(B_, "wqm", 0, D, 0, 512)
            for h in range(4):
                b = nextbank()
                for kc in range(KC):
                    mm(ps[b][:, :], wv[:, kc, h * 128:(h + 1) * 128], hs[:, kc, :], kc == 0, kc == KC - 1,
                       [("ws", s), ("hs", kc)], [("ps", b)])
                evac2(qm[:, h, :], ps[b][:, :], [("ps", b)], [("qm", h)])
            for h in range(4):
                bo = nextbank()
                bl = nextbank()
                for mc in range(2):
                    b = nextbank()
                    mm(ps[b][:, :], KmT[:, h, mc * 128:(mc + 1) * 128], qm[:, h, :], True, True,
                       [("KmT",), ("qm", h)], [("ps", b)])
                    act(PTm[mc], ps[b][:, :], AF.Exp, [("ps", b)], [("PTm", mc)], scale=SCALE)
                    mm(ps[bo][:, :], Vm[:, mc, h * 128:(h + 1) * 128], PTm[mc], mc == 0, mc == 1,
                       [("Vm",), ("PTm", mc)], [("ps", bo)])
                    mm(ps[bl][:, :], ones_bf[:, :], PTm[mc], mc == 0, mc == 1,
                       [("c", "ones"), ("PTm", mc)], [("ps", bl)])
                recip("dve", rlm, ps[bl][:, :], [("ps", bl)], [("rlm",)])
                tt("dve", om[:, h, :], ps[bo][:, :], rlm, ALU.mult, [("ps", bo), ("rlm",)], [("om", h)])
            s, wv = load_w(B_, "wom", 0, 512, 0, D)
            for c in range(KC):
                b = nextbank()
                for h in range(4):
                    mm(ps[b][:, :], wv[:, h, c * 128:(c + 1) * 128], om[:, h, :], h == 0, h == 3,
                       [("ws", s), ("om", h)], [("ps", b)])
                tt("dve", xs[:, c, :], ps[b][:, :], xs[:, c, :], ALU.add, [("ps", b), ("xs", c)], [("xs", c)])
            rmsnorm(B_, 4)
            ffn(B_, "g2", "u2", "d2")
            rmsnorm(B_, 5, fp32_out=True)
            dma("pool", outT[:, p0:p0 + TT].rearrange("(k p) t -> p k t", p=128), xs, [("xs", kc) for kc in range(KC)], [])
    P.barrier()
    P.emit(nc, st)
    st.close()
    return nc, P


def rope_tables():
    nf = 32
    pos = np.arange(SEQ)
    row = (pos // 64).astype(np.float32)
    col = (pos % 64).astype(np.float32)
    inv = (np.float32(10000.0) ** (-np.arange(nf, dtype=np.float32) / np.float32(nf))).astype(np.float32)
    d = np.arange(128)
    axis = d // 64
    pair = (d % 64) // 32
    f = d % 32
    ang = np.where(axis[:, None] == 0, row[None, :], col[None, :]).astype(np.float32) * inv[f][:, None]
    cos = np.cos(ang).astype(np.float32)
    sin = np.sin(ang).astype(np.float32)
    sin_signed = np.where(pair[:, None] == 0, -sin, sin).astype(np.float32)
    return cos, sin_signed


def t5_buckets(rel):
    nb = 16
    max_exact = 8
    ret = (rel > 0).astype(np.int32) * nb
    n = np.abs(rel)
    large = max_exact + (np.log(np.maximum(n, 1) / max_exact) / np.log(1024 / max_exact) * (nb - max_exact)).astype(np.int32)
    large = np.minimum(large, nb - 1)
    return (ret + np.where(n < max_exact, n, large)).astype(np.int32)


def const_tables():
    d = np.arange(128)
    pair = (d % 64) // 32
    partner = np.where(pair == 0, d + 32, d - 32)
    ident = np.eye(128, dtype=np.float32)
    ropeT = np.zeros((128, 128), np.float32)
    ropeT[partner, d] = 1.0
    anti = np.zeros((128, 128), np.float32)
    anti[127 - d, d] = 1.0
    cmat = np.concatenate([ident, ropeT, anti], axis=1)
    oh = np.zeros((33, 3, 384), np.float32)
    for g in range(3):
        for u in range(384):
            m = u - 191
            if abs(m) <= 64:
                bkt = int(t5_buckets(np.array([RATES[g] * m]))[0])
                oh[bkt, g, u] = 1.0
            else:
                oh[32, g, u] = NEGM * SCALE
    return cmat, oh.reshape(33, 3 * 384)


def make_in_maps(inputs):
    f = lambda a: np.ascontiguousarray(np.asarray(a, dtype=np.float32))
    x = np.asarray(inputs["x"], dtype=np.float32)
    mem = np.asarray(inputs["mem"], dtype=np.float32)
    cos, sin_s = rope_tables()
    cmat, oh = const_tables()
    gl = [inputs[k] for k in ("ffn1_norm", "mix_norm", "mem_x_norm", "mem_m_norm", "ffn2_norm")]
    gl = [np.asarray(g, np.float32).reshape(-1) for g in gl] + [np.asarray(inputs["final_norm"], np.float32).reshape(-1)]
    gains = np.concatenate([g.reshape(KC, 128).T for g in gl], axis=1)
    qkg = np.stack([np.asarray(inputs["q_norm_a"], np.float32).reshape(-1),
                    np.asarray(inputs["k_norm_a"], np.float32).reshape(-1)], axis=1)
    shared = {
        "ffn1_w_gate": f(inputs["ffn1_w_gate"][0]), "ffn1_w_up": f(inputs["ffn1_w_up"][0]),
        "ffn1_w_down": f(inputs["ffn1_w_down"][0]), "w_in": f(inputs["w_in"][0]), "w_out": f(inputs["w_out"][0]),
        "w_q_mem": f(inputs["w_q_mem"][0]), "w_kv_mem": f(inputs["w_kv_mem"][0]), "w_o_mem": f(inputs["w_o_mem"][0]),
        "ffn2_w_gate": f(inputs["ffn2_w_gate"][0]), "ffn2_w_up": f(inputs["ffn2_w_up"][0]),
        "ffn2_w_down": f(inputs["ffn2_w_down"][0]),
        "gains": f(gains), "qkg": f(qkg), "cmat": f(cmat), "rel_bias": f(inputs["rel_bias"]), "ohaug": f(oh),
    }
    maps = []
    for c in range(8):
        b, r = c // 4, c % 4
        sh = OWN * r
        m = dict(shared)
        m["xT"] = np.ascontiguousarray(np.roll(x[b], -sh, axis=0).T)
        m["memT"] = np.ascontiguousarray(mem[b].T)
        m["cosT"] = np.ascontiguousarray(np.roll(cos, -sh, axis=1))
        m["sinT"] = np.ascontiguousarray(np.roll(sin_s, -sh, axis=1))
        mr = np.zeros((1, 256), np.float32)
        if r == 0:
            mr[0, 0:64] = NEGM
        if r == 3:
            mr[0, 128 + 64:256] = NEGM
        m["mrows"] = mr
        maps.append(m)
    return maps


_NC_CACHE = {}


def kernel(**inputs):
    if "nc" not in _NC_CACHE:
        _NC_CACHE["nc"] = build()[0]
    nc = _NC_CACHE["nc"]
    maps = make_in_maps(inputs)
    res = run_bass_kernel_spmd(nc, maps, core_ids=list(range(8)))
    out = np.empty((2, SEQ, D), np.float32)
    for c in range(8):
        b, r = c // 4, c % 4
        out[b, OWN * r:OWN * (r + 1), :] = np.asarray(res.results[c]["outT"]).T
    return out
```

```python
import numpy as np
from contextlib import ExitStack
import concourse.bass as bass
import concourse.mybir as mybir
from concourse.bass_utils import run_bass_kernel_spmd

F32 = mybir.dt.float32
BF16 = mybir.dt.bfloat16
ALU = mybir.AluOpType
AF = mybir.ActivationFunctionType

D = 2048
KC = 16
DFF = 5632
FC = 44
TT = 512
SEQ = 16384
OWN = 4096
HALO = 1024
IN_W = 10752
EPS = 1e-6
SCALE = 128 ** -0.5
NEGM = -3000.0
RATES = (1, 4, 16)
NBLK = (32, 8, 2)
CH_OFF = (0, 33, 33 + 36)
NCH = (33, 36, 48)


class Prog:
    CE = ("pe", "act", "dve", "pool")
    QE = ("sp", "act", "pool")
    NPOOL = 24

    def __init__(self):
        self.ops = []
        self.lw = {}
        self.rd_c = {}
        self.rd_d = {}
        self.last_c = {}
        self.dma_all = []
        self.log = None

    def add(self, eng, fn, reads=(), writes=(), dma=False):
        i = len(self.ops)
        deps = {}
        for k in reads:
            w = self.lw.get(k)
            if w is not None:
                deps[w] = "raw"
            if k[0] == "ps":
                for e2, r in self.rd_c.get(k, {}).items():
                    if e2 != eng:
                        deps.setdefault(r, "war")
        for k in writes:
            w = self.lw.get(k)
            if w is not None:
                deps.setdefault(w, "waw")
            for r in self.rd_c.get(k, {}).values():
                deps.setdefault(r, "war")
            for r in self.rd_d.get(k, ()):
                deps.setdefault(r, "war")
        deps.pop(i, None)
        self.ops.append(dict(eng=eng, fn=fn, dma=dma, deps=deps))
        for k in reads:
            if dma:
                self.rd_d.setdefault(k, []).append(i)
            else:
                self.rd_c.setdefault(k, {})[eng] = i
        for k in writes:
            self.lw[k] = i
            self.rd_c[k] = {}
            self.rd_d[k] = []
        if dma:
            self.dma_all.append(i)
        else:
            self.last_c[eng] = i
        return i

    def barrier(self):
        prev_c = dict(self.last_c)
        prev_d = list(self.dma_all)
        for e in ("pe", "act", "dve", "pool", "sp"):
            i = len(self.ops)
            deps = {j: "raw" for j in prev_c.values()}
            for j in prev_d:
                deps[j] = "raw"
            self.ops.append(dict(eng=e, fn=None, dma=False, deps=deps))
        self.lw = {}
        self.rd_c = {}
        self.rd_d = {}

    def emit(self, nc, st):
        ops = self.ops
        qcount = {q: 0 for q in self.QE}
        for o in ops:
            if o["dma"]:
                q = o["eng"]
                k = qcount[q]
                qcount[q] += 1
                o["dk"] = k
        esem = {e: st.enter_context(nc.semaphore("se_" + e)) for e in self.CE}
        qsem = {q: [st.enter_context(nc.semaphore("sq_%s_%d" % (q, j))) for j in range(self.NPOOL)]
                for q in self.QE}
        P = self.NPOOL
        need = [False] * len(ops)
        for i, o in enumerate(ops):
            kept = {}
            best_same = {}
            for j, kind in o["deps"].items():
                oj = ops[j]
                if oj["dma"]:
                    kept[j] = kind
                    continue
                if oj["fn"] is None and oj["eng"] == "sp":
                    continue
                if (not o["dma"]) and oj["eng"] == o["eng"]:
                    if o["eng"] == "pe" or o["fn"] is None:
                        continue
                e = oj["eng"]
                if e not in best_same or best_same[e] < j:
                    best_same[e] = j
            for e, j in best_same.items():
                kept[j] = "x"
            o["kept"] = kept
            for j in kept:
                if not ops[j]["dma"]:
                    need[j] = True
        ms = {}
        cnt = {e: 0 for e in self.CE}
        for i, o in enumerate(ops):
            if not o["dma"] and need[i] and o["eng"] in cnt:
                cnt[o["eng"]] += 1
                ms[i] = cnt[o["eng"]]
        self.stats = dict(cnt)

        def semval(j):
            oj = ops[j]
            if oj["dma"]:
                k = oj["dk"]
                return qsem[oj["eng"]][k % P], 16 * (k // P + 1)
            return esem[oj["eng"]], ms[j]

        def run(engname, eng):
            waited = {}
            for i, o in enumerate(ops):
                if o["eng"] != engname:
                    continue
                ws = []
                for j in o["kept"]:
                    ws.append(semval(j))
                if o["dma"]:
                    k = o["dk"]
                    if k >= P:
                        ws.append((qsem[engname][k % P], 16 * (k // P)))
                for s, v in ws:
                    key = id(s)
                    if waited.get(key, 0) >= v:
                        continue
                    waited[key] = v
                    eng.wait_ge(s, v)
                    if self.log is not None:
                        self.log.append((engname, i, "wait", s.name, v))
                if self.log is not None:
                    self.log.append((engname, i, "op", o.get("dk"), ms.get(i)))
                if o["fn"] is None:
                    if i in ms:
                        eng.nop().then_inc(esem[engname], 1)
                    continue
                ins = o["fn"](eng)
                if o["dma"]:
                    k = o["dk"]
                    ins.then_inc(qsem[engname][k % P], 16)
                elif i in ms:
                    ins.then_inc(esem[engname], 1)

        with nc.Block() as block:
            @block.tensor
            def _(e):
                run("pe", e)

            @block.scalar
            def _(e):
                run("act", e)

            @block.vector
            def _(e):
                run("dve", e)

            @block.gpsimd
            def _(e):
                run("pool", e)

            @block.sync
            def _(e):
                run("sp", e)


class Arena:
    def __init__(self, t, nwords):
        self.t = t
        self.n = nwords
        self.off = 0

    def f32(self, n):
        assert self.off + n <= self.n, ("arena overflow", self.off, n, self.n)
        ap = self.t[:, self.off:self.off + n]
        self.off += n
        return ap

    def bf(self, n):
        w = (n + 1) // 2
        assert self.off + w <= self.n, ("arena overflow", self.off, w, self.n)
        ap = self.t[:, self.off:self.off + w].bitcast(BF16)
        self.off += w
        return ap


ARENA_WORDS = 46 * 1024


def build(cfg=None):
    cfg = cfg or {}
    NT_ALL = cfg.get("nt_all", SEQ // TT)
    NT_OWN = cfg.get("nt_own", OWN // TT)
    DBG = cfg.get("dbg", False)
    PH = cfg.get("phases", "0ABDC")
    nc = bass.Bass("TRN2", target_bir_lowering=False)
    st = ExitStack()
    P = Prog()

    def din(name, shape, dt=F32):
        return nc.dram_tensor(name, list(shape), dt, kind="ExternalInput").ap()

    DUMP = cfg.get("dump", ())

    def dscr(name, shape, dt):
        kind = "ExternalOutput" if name in DUMP else ("ExternalInput" if name in cfg.get("ext_in", ()) else "Internal")
        return nc.dram_tensor(name, list(shape), dt, kind=kind).ap()

    xT = din("xT", [D, SEQ])
    memT = din("memT", [D, 256])
    w_in32 = {
        "g1": din("ffn1_w_gate", [D, DFF]), "u1": din("ffn1_w_up", [D, DFF]), "d1": din("ffn1_w_down", [DFF, D]),
        "win": din("w_in", [D, IN_W]), "wout": din("w_out", [D, D]),
        "wqm": din("w_q_mem", [D, 512]), "wkvm": din("w_kv_mem", [D, 1024]), "wom": din("w_o_mem", [512, D]),
        "g2": din("ffn2_w_gate", [D, DFF]), "u2": din("ffn2_w_up", [D, DFF]), "d2": din("ffn2_w_down", [DFF, D]),
    }
    gains_in = din("gains", [128, 6 * KC])
    qkg_in = din("qkg", [128, 2])
    cosT = din("cosT", [128, SEQ])
    sinT = din("sinT", [128, SEQ])
    cmat_in = din("cmat", [128, 3 * 128])
    mrows_in = din("mrows", [1, 256])
    relb_in = din("rel_bias", [32, 24])
    oh_in = din("ohaug", [33, 3 * 384])
    outT = nc.dram_tensor("outT", [D, OWN], F32, kind="ExternalOutput").ap()

    wbf = {k: dscr("bf_" + k, v.shape, BF16) for k, v in w_in32.items()}
    x1T = dscr("x1T", [D, OWN], F32)
    KaT = dscr("KaT", [2, 128, SEQ], BF16)
    Va = dscr("Va", [SEQ, 256], BF16)
    QaT = dscr("QaT", [8, 128, OWN], BF16)
    QbT = dscr("QbT", [24, 128, OWN], BF16)
    KbT = dscr("KbT", [24, 128, OWN + 2 * HALO], BF16)
    Vb = dscr("Vb", [OWN + 2 * HALO, 3072], BF16)
    mixT = dscr("mixT", [D, OWN], BF16)
    fbuf = dscr("fbuf", [24, 384], BF16)

    arena_t = st.enter_context(nc.sbuf_tensor("arena", [128, ARENA_WORDS], F32))
    ones_bf = st.enter_context(nc.sbuf_tensor("ones_bf", [128, 128], BF16))
    cmat = st.enter_context(nc.sbuf_tensor("cmat_sb", [128, 3 * 128], BF16))
    gains = st.enter_context(nc.sbuf_tensor("gains_sb", [128, 6 * KC], F32))
    qkg = st.enter_context(nc.sbuf_tensor("qkg_sb", [128, 2], F32))
    mrows = st.enter_context(nc.sbuf_tensor("mrows_sb", [1, 256], BF16))
    ones_row = st.enter_context(nc.sbuf_tensor("ones_row", [1, 128], BF16))
    wones = st.enter_context(nc.sbuf_tensor("wones", [128, 512], BF16))
    relb = st.enter_context(nc.sbuf_tensor("relb_sb", [33, 24], F32))
    ohaug = st.enter_context(nc.sbuf_tensor("ohaug_sb", [33, 3 * 384], F32))
    ps = [st.enter_context(nc.psum_tensor("ps%d" % i, [128, 512], F32)) for i in range(8)]
    ident = cmat[:, 0:128]
    ropeT = cmat[:, 128:256]
    antiI = cmat[:, 256:384]
    A = Arena(arena_t, ARENA_WORDS)

    psn = [0]

    def nextbank():
        b = psn[0] % 8
        psn[0] += 1
        return b

    STQ = cfg.get("stq", "pool")

    def dma(q, out, in_, reads, writes):
        if q == "pool" and reads:
            q = STQ
        return P.add(q, lambda e, o=out, i=in_: e.dma_start(out=o, in_=i), reads=reads, writes=writes, dma=True)

    open_grp = {}

    def mm(out, lhsT, rhs, start, stop, reads, writes, gkey=None, first=None, skip=False):
        if first is None:
            first = start
        i = P.add("pe", lambda e, o=out, l=lhsT, r=rhs, s0=start, s1=stop, sk=skip: e.matmul(
            o, l, r, start=s0, stop=s1, skip_group_check=sk), reads=reads, writes=writes)
        gk = gkey if gkey is not None else tuple(writes)
        start = first
        if start:
            open_grp[gk] = []
        open_grp[gk].append(i)
        if stop:
            for j in open_grp.pop(gk):
                P.ops[j]["redir"] = i
        return i

    def act(out, in_, func, reads, writes, scale=1.0, bias=0.0):
        return P.add("act", lambda e, o=out, i=in_, f=func, s=scale, b=bias: e.activation(o, i, f, bias=b, scale=s),
                     reads=reads, writes=writes)

    def ts(eng, out, in0, s1, s2, op0, op1, reads, writes):
        if op1 is None:
            return P.add(eng, lambda e, o=out, i=in0, a=s1, p0=op0: e.tensor_scalar(o, i, a, None, p0),
                         reads=reads, writes=writes)
        return P.add(eng, lambda e, o=out, i=in0, a=s1, b=s2, p0=op0, p1=op1: e.tensor_scalar(o, i, a, b, p0, p1),
                     reads=reads, writes=writes)

    def stt(eng, out, in0, sc, in1, op0, op1, reads, writes):
        return P.add(eng, lambda e, o=out, i=in0, s=sc, j=in1, p0=op0, p1=op1: e.scalar_tensor_tensor(o, i, s, j, p0, p1),
                     reads=reads, writes=writes)

    def tt(eng, out, in0, in1, op, reads, writes):
        return P.add(eng, lambda e, o=out, i=in0, j=in1, p=op: e.tensor_tensor(o, i, j, p), reads=reads, writes=writes)

    def cp(eng, out, in_, reads, writes):
        return P.add(eng, lambda e, o=out, i=in_: e.tensor_copy(o, i), reads=reads, writes=writes)

    def transp(out, in_, reads, writes):
        return P.add("pe", lambda e, o=out, i=in_: e.transpose(o, i, ident), reads=reads + [("c", "cmat")], writes=writes)

    def recip(eng, out, in_, reads, writes):
        return P.add(eng, lambda e, o=out, i=in_: e.reciprocal(o, i), reads=reads, writes=writes)

    def memset(eng, ap, val, writes):
        return P.add(eng, lambda e, a=ap, v=val: e.memset(a, v), reads=(), writes=writes)

    memset("pool", ones_bf[:, :], 1.0, [("c", "ones")])
    memset("pool", ones_row[:, :], 1.0, [("c", "onesrow")])
    memset("pool", wones[:, :], 1.0, [("c", "wones")])
    dma("pool", cmat[:, :], cmat_in, [], [("c", "cmat")])
    dma("pool", mrows[:, :], mrows_in, [], [("c", "mrows")])
    dma("sp", gains[:, :], gains_in, [], [("c", "gains")])
    dma("sp", qkg[:, :], qkg_in, [], [("c", "qkg")])
    dma("sp", relb[0:32, :], relb_in, [], [("c", "relb")])
    memset("pool", relb[32:33, :], 1.0, [("c", "relb1")])
    dma("sp", ohaug[:, :], oh_in, [], [("c", "ohaug")])

    COLCONV = ("g1", "u1")

    def wkeys(name, r0, r1, c0=None, c1=None):
        if name in COLCONV and c0 is not None:
            return [("w", name, "c", J) for J in range(c0 // 512, (c1 + 511) // 512)]
        return [("w", name, rc) for rc in range(r0 // 128, (r1 + 127) // 128)]

    def convert_cols(names):
        for J in range(DFF // 512):
            for name in names:
                src, dst = w_in32[name], wbf[name]
                dma("pool", dst[:, J * 512:(J + 1) * 512], src[:, J * 512:(J + 1) * 512], [], [("w", name, "c", J)])

    def convert(name):
        src, dst = w_in32[name], wbf[name]
        rows, cols = src.shape
        rb = 256 if cols > 4096 else 512
        rb = min(rb, rows)
        for r0 in range(0, rows, rb):
            dma("pool", dst[r0:r0 + rb, :], src[r0:r0 + rb, :], [], wkeys(name, r0, r0 + rb))

    if "0" in PH:
        convert_cols(COLCONV)
        for name in ("d1", "win", "wout", "wqm", "wkvm", "wom", "g2", "u2", "d2"):
            convert(name)

    def phase_ffn_buffers():
        A.off = 0
        B_ = {}
        B_["xs"] = A.f32(KC * TT).rearrange("p (k t) -> p k t", t=TT)
        B_["hid"] = A.bf(FC * TT).rearrange("p (k t) -> p k t", t=TT)
        B_["hs"] = A.bf(KC * TT).rearrange("p (k t) -> p k t", t=TT)
        B_["ws"] = [A.bf(8192) for _ in range(3 if cfg.get("sqsep") else 4)]
        if cfg.get("sqsep"):
            B_["sq"] = A.bf(KC * TT).rearrange("p (k t) -> p k t", t=TT)
        B_["rstd"] = A.f32(TT)
        B_["sg"] = [A.f32(TT) for _ in range(2)]
        return B_

    wsn = [0]

    def load_w(B_, name, r0, r1, c0, c1):
        s = wsn[0] % len(B_["ws"])
        wsn[0] += 1
        nk = (r1 - r0) // 128
        ncol = c1 - c0
        dst = B_["ws"][s][:, 0:nk * ncol].rearrange("p (k c) -> p k c", c=ncol)
        src = wbf[name][r0:r1, c0:c1].rearrange("(k p) c -> p k c", p=128)
        dma("sp", dst, src, wkeys(name, r0, r1, c0, c1), [("ws", s)])
        return s, dst

    WARM1 = cfg.get("warm1", 10)
    WARM2 = cfg.get("warm2", 48)

    def warm(n):
        if n <= 0:
            return
        b = nextbank()
        for _ in range(n):
            mm(ps[b][:, :], ones_bf[:, :], wones[:, :], True, True, [("c", "ones"), ("c", "wones")], [("ps", b)])

    def rmsnorm(B_, gidx, ntok=TT, fp32_out=None):
        xs, hid, hs, rstd = B_["xs"], B_["hid"], B_["hs"], B_["rstd"]
        NS = cfg.get("nstep", 99) if gidx == 1 else 99
        hkey = "hid"
        if "sq" in B_:
            hid = B_["sq"]
            hkey = "sq"
        if NS < 1:
            return
        for q in range(4):
            rk = [("xs", kc) for kc in range(q * 4, q * 4 + 4)]
            wk = [(hkey, kc) for kc in range(q * 4, q * 4 + 4)]
            if q % 2 == 0:
                act(hid[:, q * 4:(q + 1) * 4, 0:ntok], xs[:, q * 4:(q + 1) * 4, 0:ntok], AF.Square, rk, wk)
            else:
                tt("dve", hid[:, q * 4:(q + 1) * 4, 0:ntok], xs[:, q * 4:(q + 1) * 4, 0:ntok], xs[:, q * 4:(q + 1) * 4, 0:ntok],
                   ALU.mult, rk, wk)
        warm(WARM1)
        b = nextbank()
        for kc in range(KC):
            mm(ps[b][:, 0:ntok], ones_bf[:, :], hid[:, kc, 0:ntok], kc == 0, kc == KC - 1,
               [(hkey, kc), ("c", "ones")], [("ps", b)])
        warm(WARM2)
        act(rstd[:, 0:ntok], ps[b][:, 0:ntok], AF.Sqrt, [("ps", b)], [("rstd",)], scale=1.0 / D, bias=EPS)
        recip("dve", rstd[:, 0:ntok], rstd[:, 0:ntok], [("rstd",)], [("rstd",)])
        for kc in range(KC):
            eng = "dve"
            if fp32_out is None:
                stt(eng, hs[:, kc, 0:ntok], xs[:, kc, 0:ntok], gains[:, gidx * KC + kc:gidx * KC + kc + 1], rstd[:, 0:ntok],
                    ALU.mult, ALU.mult, [("xs", kc), ("rstd",), ("c", "gains")], [("hs", kc)])
            else:
                stt(eng, xs[:, kc, 0:ntok], xs[:, kc, 0:ntok], gains[:, gidx * KC + kc:gidx * KC + kc + 1], rstd[:, 0:ntok],
                    ALU.mult, ALU.mult, [("xs", kc), ("rstd",), ("c", "gains")], [("xs", kc)])

    def ffn(B_, wg, wu, wd):
        xs, hid, hs, sgt = B_["xs"], B_["hid"], B_["hs"], B_["sg"]
        for J in range(FC // 4):
            s_g, wgv = load_w(B_, wg, 0, D, J * 512, (J + 1) * 512)
            s_u, wuv = load_w(B_, wu, 0, D, J * 512, (J + 1) * 512)
            for jj in range(4):
                j = J * 4 + jj
                bg = nextbank()
                bu = nextbank()
                for kc in range(KC):
                    mm(ps[bg][:, :], wgv[:, kc, jj * 128:(jj + 1) * 128], hs[:, kc, :], kc == 0, kc == KC - 1,
                       [("ws", s_g), ("hs", kc)], [("ps", bg)])
                for kc in range(KC):
                    mm(ps[bu][:, :], wuv[:, kc, jj * 128:(jj + 1) * 128], hs[:, kc, :], kc == 0, kc == KC - 1,
                       [("ws", s_u), ("hs", kc)], [("ps", bu)])
                si = j % 2
                act(sgt[si], ps[bg][:, :], AF.Silu, [("ps", bg)], [("sg", si)])
                tt("dve", hid[:, j, :], sgt[si], ps[bu][:, :], ALU.mult, [("sg", si), ("ps", bu)], [("hid", j)])
        for N in range(4):
            banks = [nextbank() for _ in range(4)]
            for q in range(4):
                s_d, wdv = load_w(B_, wd, q * 11 * 128, (q + 1) * 11 * 128, N * 512, (N + 1) * 512)
                for kl in range(11):
                    k = q * 11 + kl
                    for n4 in range(4):
                        mm(ps[banks[n4]][:, :], wdv[:, kl, n4 * 128:(n4 + 1) * 128], hid[:, k, :], k == 0, k == FC - 1,
                           [("ws", s_d), ("hid", k)], [("ps", banks[n4])])
            for n4 in range(4):
                c = N * 4 + n4
                stt("dve", xs[:, c, :], ps[banks[n4]][:, :], 0.5, xs[:, c, :], ALU.mult, ALU.add,
                    [("ps", banks[n4]), ("xs", c)], [("xs", c)])

    if "A" in PH:
        B_ = phase_ffn_buffers()
        xs, hs = B_["xs"], B_["hs"]
        ctab = A.f32(TT)
        stab = A.f32(TT)
        t1 = A.f32(TT)
        t2 = A.f32(TT)
        rs2 = A.f32(TT)
        kgb = A.bf(TT)
        sqk = A.bf(TT)
        kr = [A.bf(TT) for _ in range(2)]
        stg = [A.bf(TT) for _ in range(3)]
        vst = [A.bf(4 * 256).rearrange("p (a c) -> p a c", c=256) for _ in range(2)]
        cnt = {"kr": 0, "stg": 0, "vst": 0, "cp": 0}

        def normrope(b, gi, dst):
            NR = cfg.get("nr", 99)
            if NR < 1:
                return
            act(sqk, ps[b][:, :], AF.Square, [("ps", b)], [("sqk",)])
            if NR < 2:
                return
            ts("dve", kgb, ps[b][:, :], qkg[:, gi:gi + 1], None, ALU.mult, None, [("ps", b), ("c", "qkg")], [("kgb",)])
            if NR < 3:
                return
            b2 = nextbank()
            mm(ps[b2][:, :], ones_bf[:, :], sqk, True, True, [("sqk",), ("c", "ones")], [("ps", b2)])
            if NR < 4:
                return
            b3 = nextbank()
            mm(ps[b3][:, :], ropeT, kgb, True, True, [("kgb",), ("c", "cmat")], [("ps", b3)])
            if NR < 5:
                return
            VAR = cfg.get("var", 0)
            if VAR == 1:
                act(rs2, ps[b2][:, :], AF.Copy, [("ps", b2)], [("rs2",)])
            elif VAR == 2:
                act(rs2, ps[b2][:, :], AF.Sqrt, [("ps", b2)], [("rs2",)], scale=1.0 / 128, bias=EPS)
                return
            elif VAR == 3:
                act(rs2, ps[b2][:, :], AF.Sqrt, [("ps", b2)], [("rs2",)], scale=1.0 / D, bias=EPS)
            else:
                act(rs2, ps[b2][:, :], AF.Sqrt, [("ps", b2)], [("rs2",)], scale=1.0 / 128, bias=EPS)
            recip("dve", rs2, rs2, [("rs2",)], [("rs2",)])
            if NR < 6:
                return
            stt("dve", t1, ps[b][:, :], qkg[:, gi:gi + 1], ctab, ALU.mult, ALU.mult, [("ps", b), ("ctab",), ("c", "qkg")], [("t1",)])
            tt("dve", t2, ps[b3][:, :], stab, ALU.mult, [("ps", b3), ("stab",)], [("t2",)])
            tt("dve", t1, t1, t2, ALU.add, [("t1",), ("t2",)], [("t1",)])
            if NR < 7:
                return
            si = cnt["kr"] % 2
            cnt["kr"] += 1
            tt("dve", kr[si], t1, rs2, ALU.mult, [("t1",), ("rs2",)], [("kr", si)])
            if NR < 8:
                return
            dma("pool", dst, kr[si], [("kr", si)], [])

        def evac_copy(out, in_, reads, writes):
            cnt["cp"] += 1
            if cnt["cp"] % 2:
                act(out, in_, AF.Copy, reads, writes)
            else:
                cp("dve", out, in_, reads, writes)

        def proj_fm_heads(c0, nheads, dst_fn):
            for h0 in range(0, nheads, 4):
                s, wv = load_w(B_, "win", 0, D, c0 + h0 * 128, c0 + (h0 + 4) * 128)
                for hh in range(4):
                    b = nextbank()
                    for kc in range(KC):
                        mm(ps[b][:, :], wv[:, kc, hh * 128:(hh + 1) * 128], hs[:, kc, :], kc == 0, kc == KC - 1,
                           [("ws", s), ("hs", kc)], [("ps", b)])
                    dst_fn(h0 + hh, b)

        def plain_store(dst):
            def f(b):
                si = cnt["stg"] % 3
                cnt["stg"] += 1
                evac_copy(stg[si], ps[b][:, :], [("ps", b)], [("stg", si)])
                dma("pool", dst, stg[si], [("stg", si)], [])
            return f

        for ti in list(range(NT_ALL)):
            p0 = ti * TT
            own = ti < NT_OWN
            halo = (NT_OWN <= ti < NT_OWN + 2) or ti >= NT_ALL - 2
            pos0 = p0 if ti < NT_OWN + 2 else p0 - NT_ALL * TT
            dma("sp", xs, xT[:, p0:p0 + TT].rearrange("(k p) t -> p k t", p=128), [], [("xs", kc) for kc in range(KC)])
            STOP = cfg.get("stop", 99)
            if STOP < 1:
                continue
            rmsnorm(B_, 0)
            if STOP < 2:
                continue
            ffn(B_, "g1", "u1", "d1")
            if STOP < 3:
                continue
            if own:
                dma("pool", x1T[:, p0:p0 + TT].rearrange("(k p) t -> p k t", p=128), xs, [("xs", kc) for kc in range(KC)], [])
            if STOP < 3.5:
                continue
            rmsnorm(B_, 1)
            if STOP < 4:
                continue
            dma("sp", ctab, cosT[:, p0:p0 + TT], [], [("ctab",)])
            dma("sp", stab, sinT[:, p0:p0 + TT], [], [("stab",)])
            s, wv = load_w(B_, "win", 0, D, 1024, 1536)
            for hd in range(2):
                b = nextbank()
                for kc in range(KC):
                    mm(ps[b][:, :], wv[:, kc, hd * 128:(hd + 1) * 128], hs[:, kc, :], kc == 0, kc == KC - 1,
                       [("ws", s), ("hs", kc)], [("ps", b)])
                normrope(b, 1, KaT[hd, :, p0:p0 + TT])
            if STOP < 4.1:
                continue
            vi = cnt["vst"] % 2
            cnt["vst"] += 1
            for tb in range(4):
                b = nextbank()
                for kc in range(KC):
                    mm(ps[b][:, 0:256], hs[:, kc, tb * 128:(tb + 1) * 128], wv[:, kc, 256:512], kc == 0, kc == KC - 1,
                       [("ws", s), ("hs", kc)], [("ps", b)])
                evac_copy(vst[vi][:, tb, :], ps[b][:, 0:256], [("ps", b)], [("vst", vi)])
            dma("pool", Va[p0:p0 + TT, :].rearrange("(a p) c -> p a c", p=128), vst[vi], [("vst", vi)], [])
            if STOP < 4.2:
                continue
            if own:
                proj_fm_heads(0, 8, lambda h, b: normrope(b, 0, QaT[h, :, p0:p0 + TT]))
                if STOP < 4.3:
                    continue
                proj_fm_heads(1536, 24, lambda h, b: plain_store(QbT[h, :, p0:p0 + TT])(b))
            if STOP < 4.4:
                continue
            if own or halo:
                c0 = pos0 + HALO
                proj_fm_heads(4608, 24, lambda h, b: plain_store(KbT[h, :, c0:c0 + TT])(b))
                if STOP < 4.5:
                    continue
                for i6 in range(6):
                    s, wv = load_w(B_, "win", 0, D, 7680 + i6 * 512, 7680 + (i6 + 1) * 512)
                    for tb in range(4):
                        b = nextbank()
                        for kc in range(KC):
                            mm(ps[b][:, :], hs[:, kc, tb * 128:(tb + 1) * 128], wv[:, kc, :], kc == 0, kc == KC - 1,
                               [("ws", s), ("hs", kc)], [("ps", b)])
                        plain_store(Vb[c0 + tb * 128:c0 + (tb + 1) * 128, i6 * 512:(i6 + 1) * 512])(b)
        P.barrier()


    OWN_T = NT_OWN * TT
    if "B" in PH:
        A.off = 0
        NKC = NT_ALL * 4
        NQB = NT_OWN * 4
        SK = NT_ALL * TT
        Kt = [A.bf(SK) for _ in range(2)]
        Vt = [A.bf(NKC * 129).rearrange("p (c d) -> p c d", d=129) for _ in range(2)]
        Qt = [A.bf(512) for _ in range(2)]
        PT = [A.bf(512) for _ in range(4)]
        osb = [A.bf(512).rearrange("p (h d) -> p h d", d=128) for _ in range(2)]
        rl = [A.f32(4) for _ in range(2)]
        mst = [A.bf(512) for _ in range(2)]
        KPC = 8
        for g in range(2):
            memset("pool", Vt[g][:, :, 128:129], 1.0, [("V1", g)])
            for pc in range(NKC // KPC):
                c0 = pc * KPC
                dma("sp", Kt[g][:, c0 * 128:(c0 + KPC) * 128], KaT[g, :, c0 * 128:(c0 + KPC) * 128], [], [("K", g, pc)])
                dma("sp", Vt[g][:, c0:c0 + KPC, 0:128],
                    Va[c0 * 128:(c0 + KPC) * 128, g * 128:(g + 1) * 128].rearrange("(c p) d -> p c d", p=128),
                    [], [("V", g, pc)])
        its = [(g, qb, kc) for g in range(2) for qb in range(NQB) for kc in range(NKC)]
        sbank = [0]

        def issue_S(idx):
            g, qb, kc = its[idx]
            blk = g * NQB + qb
            qs = blk % 2
            if kc == 0:
                dma("sp", Qt[qs].rearrange("p (h q) -> p h q", q=128),
                    QaT[4 * g:4 * g + 4, :, qb * 128:(qb + 1) * 128].rearrange("h d q -> d h q"), [], [("Qt", qs)])
            b = sbank[0] % 4
            sbank[0] += 1
            mm(ps[b][:, :], Kt[g][:, kc * 128:(kc + 1) * 128], Qt[qs], True, True,
               [("K", g, kc // KPC), ("Qt", qs)], [("ps", b)])
            pi = idx % 4
            act(PT[pi], ps[b][:, :], AF.Exp, [("ps", b)], [("PT", pi)], scale=SCALE)

        def issue_PV(idx):
            g, qb, kc = its[idx]
            blk = g * NQB + qb
            par = blk % 2
            pi = idx % 4
            for hh in range(4):
                bank = 4 + 2 * par + hh // 2
                c0 = (hh % 2) * 256
                mm(ps[bank][:, c0:c0 + 129], PT[pi][:, hh * 128:(hh + 1) * 128], Vt[g][:, kc, :],
                   (kc == 0 and hh % 2 == 0), kc == NKC - 1,
                   [("PT", pi), ("V", g, kc // KPC), ("V1", g)], [("ps", bank)], gkey=("acc", par, hh), first=(kc == 0), skip=True)
            if kc == NKC - 1:
                for hh in range(4):
                    bank = 4 + 2 * par + hh // 2
                    c0 = (hh % 2) * 256
                    recip("dve", rl[par][:, hh:hh + 1], ps[bank][:, c0 + 128:c0 + 129], [("ps", bank)], [("rl", par, hh)])
                    ts("dve", osb[par][:, hh, :], ps[bank][:, c0:c0 + 128], rl[par][:, hh:hh + 1], None, ALU.mult, None,
                       [("ps", bank), ("rl", par, hh)], [("osb", par, hh)])
                b = sbank[0] % 4
                sbank[0] += 1
                pst = ps[b].bitcast(BF16)
                for hh in range(4):
                    transp(pst[:, hh * 128:(hh + 1) * 128], osb[par][:, hh, :], [("osb", par, hh)], [("ps", b)])
                cp("dve", mst[par], pst[:, 0:512], [("ps", b)], [("mst", par)])
                dma("pool", mixT[4 * g * 128:(4 * g + 4) * 128, qb * 128:(qb + 1) * 128].rearrange("(h d) q -> d h q", d=128),
                    mst[par].rearrange("p (h q) -> p h q", q=128), [("mst", par)], [])

        LA = 2
        for idx in range(min(LA, len(its))):
            issue_S(idx)
        for idx in range(len(its)):
            if idx + LA < len(its):
                issue_S(idx + LA)
            issue_PV(idx)
        P.barrier()

    if "D" in PH:
        A.off = 0
        KW = OWN_T + 2 * HALO
        nblk = [OWN_T // (128 * r) for r in RATES]
        fb_sb = A.bf(384)
        for g in range(3):
            b = nextbank()
            mm(ps[b][0:8, 0:384], relb[0:33, g * 8:(g + 1) * 8], ohaug[0:33, g * 384:(g + 1) * 384], True, True,
               [("c", "relb"), ("c", "relb1"), ("c", "ohaug")], [("ps", b)])
            act(fb_sb[0:8, :], ps[b][0:8, 0:384], AF.Copy, [("ps", b)], [("fb",)], scale=1.0 / SCALE)
            dma("pool", fbuf[g * 8:(g + 1) * 8, :], fb_sb[0:8, :], [("fb",)], [("fbuf", g)])
        Hsb = A.bf(24 * 2 * 128).rearrange("p (a c k) -> p a c k", c=2, k=128)
        fb_t = fbuf.tensor
        for gh in range(24):
            for cc in range(2):
                src = bass.AP(tensor=fb_t, offset=gh * 384 + 128 * cc, ap=[[1, 128], [1, 128]])
                dma("sp", Hsb[:, gh, cc, :], src, [("fbuf", gh // 8)], [("H", gh)])
        Kb_sb = [A.bf(KW) for _ in range(2)]
        Qb_sb = [A.bf(OWN_T) for _ in range(2)]
        maxch = max(r * (nb + 1) for r, nb in zip(RATES, nblk))
        Vb_sb2 = [[A.bf(maxch * 128).rearrange("p (c d) -> p c d", d=128) for _ in range(2)] for _ in range(2)]
        itn = [0]
        accb = A.f32(4 * OWN_T).rearrange("p (a t) -> p a t", t=OWN_T)
        PTb = [A.bf(512) for _ in range(3)]
        obf1 = A.bf(OWN_T)
        obf = [obf1, obf1]
        ptn = [0]
        for hp in range(4):
            for g in range(3):
                r = RATES[g]
                nb = nblk[g]
                vpar = itn[0] % 2
                itn[0] += 1
                Vb_sb = Vb_sb2[vpar]
                for hh in range(2):
                    gh = g * 8 + hp * 2 + hh
                    for rho in range(r):
                        base = HALO - 64 * r + rho
                        nrow = 128 * (nb + 1)
                        src = Vb[base:base + r * (nrow - 1) + 1:r, gh * 128:(gh + 1) * 128].rearrange("(j k) d -> k j d", k=128)
                        dma("sp", Vb_sb[hh][:, rho * (nb + 1):(rho + 1) * (nb + 1), :], src, [], [("Vb", vpar, hh)])
                for hh in range(2):
                    gh = g * 8 + hp * 2 + hh
                    dma("sp", Kb_sb[hh], KbT[gh, :, 0:KW], [], [("Kb", hh)])
                    dma("sp", Qb_sb[hh], QbT[gh, :, 0:OWN_T], [], [("Qb", hh)])
                for hh in range(2):
                    gh = g * 8 + hp * 2 + hh
                    for rho in range(r):
                        for j2 in range(nb // 2):
                            bS = nextbank()
                            bO = nextbank()
                            firstS = True
                            for bb in range(2):
                                j = 2 * j2 + bb
                                for cc in range(2):
                                    sub = ps[bS][:, (bb * 2 + cc) * 128:(bb * 2 + cc + 1) * 128]
                                    kb0 = HALO + rho + r * (128 * (j + cc) - 64)
                                    q0 = rho + r * 128 * j
                                    mm(sub, Kb_sb[hh][:, kb0:kb0 + 127 * r + 1:r], Qb_sb[hh][:, q0:q0 + 127 * r + 1:r], firstS, False,
                                       [("Kb", hh), ("Qb", hh)], [("ps", bS)], gkey=("S", bb, cc), first=True, skip=True)
                                    firstS = False
                                    edge_l = (j == 0 and cc == 0)
                                    edge_r = (j == nb - 1 and cc == 1)
                                    mm(sub, Hsb[:, gh, cc, :], antiI, False, not (edge_l or edge_r),
                                       [("H", gh), ("c", "cmat")], [("ps", bS)], gkey=("S", bb, cc), first=False, skip=True)
                                    if edge_l:
                                        mm(sub, mrows[0:1, 0:128], ones_row[0:1, :], False, not edge_r,
                                           [("c", "mrows"), ("c", "onesrow")], [("ps", bS)], gkey=("S", bb, cc), first=False, skip=True)
                                    if edge_r:
                                        mm(sub, mrows[0:1, 128:256], ones_row[0:1, :], False, True,
                                           [("c", "mrows"), ("c", "onesrow")], [("ps", bS)], gkey=("S", bb, cc), first=False, skip=True)
                            pi = ptn[0] % 3
                            ptn[0] += 1
                            act(PTb[pi], ps[bS][:, :], AF.Exp, [("ps", bS)], [("PTb", pi)], scale=SCALE)
                            firstO = True
                            for bb in range(2):
                                j = 2 * j2 + bb
                                for cc in range(2):
                                    ch = rho * (nb + 1) + j + cc
                                    pt = PTb[pi][:, (bb * 2 + cc) * 128:(bb * 2 + cc + 1) * 128]
                                    mm(ps[bO][:, bb * 128:(bb + 1) * 128], Vb_sb[hh][:, ch, :], pt, firstO, cc == 1,
                                       [("Vb", vpar, hh), ("PTb", pi)], [("ps", bO)], gkey=("O", bb), first=(cc == 0), skip=True)
                                    firstO = False
                                    mm(ps[bO][:, 256 + bb * 128:256 + (bb + 1) * 128], ones_bf[:, :], pt, False, cc == 1,
                                       [("c", "ones"), ("PTb", pi)], [("ps", bO)], gkey=("L", bb), first=(cc == 0), skip=True)
                            q0 = rho + r * 128 * (2 * j2)
                            dst = accb[:, hh:4:2, q0:q0 + 255 * r + 1:r]
                            srcp = ps[bO][:, :].rearrange("p (a q) -> p a q", q=256)
                            if g == 0:
                                cp("dve", dst, srcp, [("ps", bO)], [("accb", hh)])
                            else:
                                tt("dve", dst, srcp, dst, ALU.add, [("ps", bO), ("accb", hh)], [("accb", hh)])
            for hh in range(2):
                recip("dve", accb[:, 2 + hh, :], accb[:, 2 + hh, :], [("accb", hh)], [("accb", hh)])
                tt("dve", obf[hh], accb[:, hh, :], accb[:, 2 + hh, :], ALU.mult, [("accb", hh)], [("obf", 0)])
                h = hp * 2 + hh
                dma("pool", mixT[1024 + h * 128:1024 + (h + 1) * 128, 0:OWN_T], obf[hh], [("obf", 0)], [])
        P.barrier()

    if "C" in PH:
        B_ = phase_ffn_buffers()
        xs, hs, hid = B_["xs"], B_["hs"], B_["hid"]
        KmT = A.bf(4 * 256).rearrange("p (h m) -> p h m", m=256)
        Vm = A.bf(2 * 512).rearrange("p (c n) -> p c n", n=512)
        qm = A.bf(4 * TT).rearrange("p (h t) -> p h t", t=TT)
        om = A.bf(4 * TT).rearrange("p (h t) -> p h t", t=TT)
        PTm = [A.bf(TT) for _ in range(2)]
        rlm = A.f32(TT)
        cpn = [0]

        def evac2(out, in_, reads, writes):
            cpn[0] += 1
            if cpn[0] % 2:
                act(out, in_, AF.Copy, reads, writes)
            else:
                cp("dve", out, in_, reads, writes)

        dma("sp", xs[:, :, 0:256], memT.rearrange("(k p) t -> p k t", p=128), [], [("xs", kc) for kc in range(KC)])
        rmsnorm(B_, 3, ntok=256)
        s, wv = load_w(B_, "wkvm", 0, D, 0, 512)
        for h in range(4):
            b = nextbank()
            for kc in range(KC):
                mm(ps[b][:, 0:256], wv[:, kc, h * 128:(h + 1) * 128], hs[:, kc, 0:256], kc == 0, kc == KC - 1,
                   [("ws", s), ("hs", kc)], [("ps", b)])
            evac2(KmT[:, h, :], ps[b][:, 0:256], [("ps", b)], [("KmT",)])
        s, wv = load_w(B_, "wkvm", 0, D, 512, 1024)
        for mc in range(2):
            b = nextbank()
            for kc in range(KC):
                mm(ps[b][:, :], hs[:, kc, mc * 128:(mc + 1) * 128], wv[:, kc, :], kc == 0, kc == KC - 1,
                   [("ws", s), ("hs", kc)], [("ps", b)])
            evac2(Vm[:, mc, :], ps[b][:, :], [("ps", b)], [("Vm",)])

        for ti in range(NT_OWN):
            p0 = ti * TT
            dma("sp", xs, x1T[:, p0:p0 + TT].rearrange("(k p) t -> p k t", p=128), [], [("xs", kc) for kc in range(KC)])
            dma("sp", hs, mixT[:, p0:p0 + TT].rearrange("(k p) t -> p k t", p=128), [], [("hs", kc) for kc in range(KC)])
            for N in range(4):
                s, wv = load_w(B_, "wout", 0, D, N * 512, (N + 1) * 512)
                for n4 in range(4):
                    c = N * 4 + n4
                    b = nextbank()
                    for kc in range(KC):
                        mm(ps[b][:, :], wv[:, kc, n4 * 128:(n4 + 1) * 128], hs[:, kc, :], kc == 0, kc == KC - 1,
                           [("ws", s), ("hs", kc)], [("ps", b)])
                    tt("dve", xs[:, c, :], ps[b][:, :], xs[:, c, :], ALU.add, [("ps", b), ("xs", c)], [("xs", c)])
            rmsnorm(B_, 2)
            s, wv = load_w(B_, "wqm", 0, D, 0, 512)
            for h in range(4):
                b = nextbank()
                for kc in range(KC):
                    mm(ps[b][:, :], wv[:, kc, h * 128:(h + 1) * 128], hs[:, kc, :], kc == 0, kc == KC - 1,
                       [("ws", s), ("hs", kc)], [("ps", b)])
                evac2(qm[:, h, :], ps[b][:, :], [("ps", b)], [("qm", h)])
            for h in range(4):
                bo = nextbank()
                bl = nextbank()
                for mc in range(2):
                    b = nextbank()
                    mm(ps[b][:, :], KmT[:, h, mc * 128:(mc + 1) * 128], qm[:, h, :], True, True,
                       [("KmT",), ("qm", h)], [("ps", b)])
                    act(PTm[mc], ps[b][:, :], AF.Exp, [("ps", b)], [("PTm", mc)], scale=SCALE)
                    mm(ps[bo][:, :], Vm[:, mc, h * 128:(h + 1) * 128], PTm[mc], mc == 0, mc == 1,
                       [("Vm",), ("PTm", mc)], [("ps", bo)])
                    mm(ps[bl][:, :], ones_bf[:, :], PTm[mc], mc == 0, mc == 1,
                       [("c", "ones"), ("PTm", mc)], [("ps", bl)])
                recip("dve", rlm, ps[bl][:, :], [("ps", bl)], [("rlm",)])
                tt("dve", om[:, h, :], ps[bo][:, :], rlm, ALU.mult, [("ps", bo), ("rlm",)], [("om", h)])
            s, wv = load_w(B_, "wom", 0, 512, 0, D)
            for c in range(KC):
                b = nextbank()
                for h in range(4):
                    mm(ps[b][:, :], wv[:, h, c * 128:(c + 1) * 128], om[:, h, :], h == 0, h == 3,
                       [("ws", s), ("om", h)], [("ps", b)])
                tt("dve", xs[:, c, :], ps[b][:, :], xs[:, c, :], ALU.add, [("ps", b), ("xs", c)], [("xs", c)])
            rmsnorm(B_, 4)
            ffn(B_, "g2", "u2", "d2")
            rmsnorm(B_, 5, fp32_out=True)
            dma("pool", outT[:, p0:p0 + TT].rearrange("(k p) t -> p k t", p=128), xs, [("xs", kc) for kc in range(KC)], [])
    P.barrier()
    P.emit(nc, st)
    st.close()
    return nc, P


def rope_tables():
    nf = 32
    pos = np.arange(SEQ)
    row = (pos // 64).astype(np.float32)
    col = (pos % 64).astype(np.float32)
    inv = (np.float32(10000.0) ** (-np.arange(nf, dtype=np.float32) / np.float32(nf))).astype(np.float32)
    d = np.arange(128)
    axis = d // 64
    pair = (d % 64) // 32
    f = d % 32
    ang = np.where(axis[:, None] == 0, row[None, :], col[None, :]).astype(np.float32) * inv[f][:, None]
    cos = np.cos(ang).astype(np.float32)
    sin = np.sin(ang).astype(np.float32)
    sin_signed = np.where(pair[:, None] == 0, -sin, sin).astype(np.float32)
    return cos, sin_signed


def t5_buckets(rel):
    nb = 16
    max_exact = 8
    ret = (rel > 0).astype(np.int32) * nb
    n = np.abs(rel)
    large = max_exact + (np.log(np.maximum(n, 1) / max_exact) / np.log(1024 / max_exact) * (nb - max_exact)).astype(np.int32)
    large = np.minimum(large, nb - 1)
    return (ret + np.where(n < max_exact, n, large)).astype(np.int32)


def const_tables():
    d = np.arange(128)
    pair = (d % 64) // 32
    partner = np.where(pair == 0, d + 32, d - 32)
    ident = np.eye(128, dtype=np.float32)
    ropeT = np.zeros((128, 128), np.float32)
    ropeT[partner, d] = 1.0
    anti = np.zeros((128, 128), np.float32)
    anti[127 - d, d] = 1.0
    cmat = np.concatenate([ident, ropeT, anti], axis=1)
    oh = np.zeros((33, 3, 384), np.float32)
    for g in range(3):
        for u in range(384):
            m = u - 191
            if abs(m) <= 64:
                bkt = int(t5_buckets(np.array([RATES[g] * m]))[0])
                oh[bkt, g, u] = 1.0
            else:
                oh[32, g, u] = NEGM * SCALE
    return cmat, oh.reshape(33, 3 * 384)


def make_in_maps(inputs):
    f = lambda a: np.ascontiguousarray(np.asarray(a, dtype=np.float32))
    x = np.asarray(inputs["x"], dtype=np.float32)
    mem = np.asarray(inputs["mem"], dtype=np.float32)
    cos, sin_s = rope_tables()
    cmat, oh = const_tables()
    gl = [inputs[k] for k in ("ffn1_norm", "mix_norm", "mem_x_norm", "mem_m_norm", "ffn2_norm")]
    gl = [np.asarray(g, np.float32).reshape(-1) for g in gl] + [np.asarray(inputs["final_norm"], np.float32).reshape(-1)]
    gains = np.concatenate([g.reshape(KC, 128).T for g in gl], axis=1)
    qkg = np.stack([np.asarray(inputs["q_norm_a"], np.float32).reshape(-1),
                    np.asarray(inputs["k_norm_a"], np.float32).reshape(-1)], axis=1)
    shared = {
        "ffn1_w_gate": f(inputs["ffn1_w_gate"][0]), "ffn1_w_up": f(inputs["ffn1_w_up"][0]),
        "ffn1_w_down": f(inputs["ffn1_w_down"][0]), "w_in": f(inputs["w_in"][0]), "w_out": f(inputs["w_out"][0]),
        "w_q_mem": f(inputs["w_q_mem"][0]), "w_kv_mem": f(inputs["w_kv_mem"][0]), "w_o_mem": f(inputs["w_o_mem"][0]),
        "ffn2_w_gate": f(inputs["ffn2_w_gate"][0]), "ffn2_w_up": f(inputs["ffn2_w_up"][0]),
        "ffn2_w_down": f(inputs["ffn2_w_down"][0]),
        "gains": f(gains), "qkg": f(qkg), "cmat": f(cmat), "rel_bias": f(inputs["rel_bias"]), "ohaug": f(oh),
    }
    maps = []
    for c in range(8):
        b, r = c // 4, c % 4
        sh = OWN * r
        m = dict(shared)
        m["xT"] = np.ascontiguousarray(np.roll(x[b], -sh, axis=0).T)
        m["memT"] = np.ascontiguousarray(mem[b].T)
        m["cosT"] = np.ascontiguousarray(np.roll(cos, -sh, axis=1))
        m["sinT"] = np.ascontiguousarray(np.roll(sin_s, -sh, axis=1))
        mr = np.zeros((1, 256), np.float32)
        if r == 0:
            mr[0, 0:64] = NEGM
        if r == 3:
            mr[0, 128 + 64:256] = NEGM
        m["mrows"] = mr
        maps.append(m)
    return maps


_NC_CACHE = {}


def kernel(**inputs):
    if "nc" not in _NC_CACHE:
        _NC_CACHE["nc"] = build()[0]
    nc = _NC_CACHE["nc"]
    maps = make_in_maps(inputs)
    res = run_bass_kernel_spmd(nc, maps, core_ids=list(range(8)))
    out = np.empty((2, SEQ, D), np.float32)
    for c in range(8):
        b, r = c // 4, c % 4
        out[b, OWN * r:OWN * (r + 1), :] = np.asarray(res.results[c]["outT"]).T
    return out
```
